# Optimizing a Trainium2 kernel written in Bass

```python
import math
import jax, jax.numpy as jnp
from jax import lax
import numpy as np

D_MODEL = 1024
BATCH = 8
SEQ = 4096
DEPTH = 2

D_MIX = D_MODEL
GROUP_W = D_MIX // 4
POOL_WINDOWS = (2, 4, 8, 16)
POOL_GROUPS = len(POOL_WINDOWS)
POOL_CH = GROUP_W // POOL_GROUPS
SB_HEADS = 4
SB_HEAD_DIM = GROUP_W // SB_HEADS
SB_BLOCK = 128
SGU_HEADS = 4
SGU_HEAD_DIM = GROUP_W // SGU_HEADS
SGU_CHUNK = 128
LRU_BLOCKS = 4
LRU_BLOCK_DIM = GROUP_W // LRU_BLOCKS
CONV_WIDTH = 4
LRU_C = 8.0
D_FF = 2816
OFF_POOL = 0
OFF_Q = OFF_POOL + GROUP_W
OFF_K = OFF_Q + GROUP_W
OFF_V = OFF_K + GROUP_W
OFF_SGU_U = OFF_V + GROUP_W
OFF_SGU_V = OFF_SGU_U + GROUP_W
OFF_LRU_X = OFF_SGU_V + GROUP_W
OFF_LRU_G = OFF_LRU_X + GROUP_W
D_IN = OFF_LRU_G + GROUP_W
EPS = 1e-6

kernel_name = "hybrid_parallel_groups_pool_sb_sgu_rglru_macaron"


def rms_norm(x, g):
    x32 = x.astype(jnp.float32)
    y = x32 * lax.rsqrt(jnp.mean(x32 * x32, axis=-1, keepdims=True) + EPS)
    return (y * g.astype(jnp.float32)).astype(x.dtype)


def swiglu_ffn(x, w_in, w_out):
    gate, up = jnp.split(x @ w_in, 2, axis=-1)
    return (jax.nn.silu(gate) * up) @ w_out


def pool_mixer(xp, w, scale):
    B, S, _ = xp.shape
    x32 = xp.astype(jnp.float32).reshape(B, S, POOL_GROUPS, POOL_CH)
    cs = jnp.cumsum(x32, axis=1)
    counts_base = jnp.arange(1, S + 1, dtype=jnp.int32)
    means = []
    for g, win in enumerate(POOL_WINDOWS):
        csg = cs[:, :, g]
        prev = jnp.pad(csg, ((0, 0), (win, 0), (0, 0)))[:, :S]
        cnt = jnp.minimum(counts_base, win).astype(jnp.float32)[None, :, None]
        means.append((csg - prev) / cnt)
    d = jnp.stack(means, axis=2) - x32
    y = jnp.einsum('bsgc,gcd->bsgd', d, w.astype(jnp.float32)).reshape(B, S, GROUP_W)
    return y * scale.astype(jnp.float32)


def stick_breaking_attention(q, k, v):
    B, S, _ = q.shape
    to_heads = lambda a: a.astype(jnp.float32).reshape(B, S, SB_HEADS, SB_HEAD_DIM).transpose(0, 2, 1, 3)
    q32, k32, v32 = to_heads(q), to_heads(k), to_heads(v)
    scale = 1.0 / math.sqrt(SB_HEAD_DIM)
    key_pos = jnp.arange(S)
    n_blocks = S // SB_BLOCK

    def block(i):
        start = i * SB_BLOCK
        qb = lax.dynamic_slice_in_dim(q32, start, SB_BLOCK, axis=2)
        z = jnp.einsum('bhqd,bhkd->bhqk', qb, k32) * scale
        q_pos = start + jnp.arange(SB_BLOCK)
        mask = key_pos[None, :] < q_pos[:, None]
        log_keep = jnp.where(mask, jax.nn.log_sigmoid(-z), 0.0)
        suffix_incl = jnp.flip(jnp.cumsum(jnp.flip(log_keep, -1), axis=-1), -1)
        suffix_excl = suffix_incl - log_keep
        a = jnp.where(mask, jnp.exp(jax.nn.log_sigmoid(z) + suffix_excl), 0.0)
        return jnp.einsum('bhqk,bhkd->bhqd', a, v32)

    out = lax.map(block, jnp.arange(n_blocks))
    return out.transpose(1, 0, 3, 2, 4).reshape(B, S, GROUP_W)


def spatial_gating(u, v, w_s, b_s):
    B, S, _ = u.shape
    n_chunks = S // SGU_CHUNK
    u32 = jax.nn.gelu(u.astype(jnp.float32))
    v32 = jax.nn.gelu(v.astype(jnp.float32)).reshape(B, n_chunks, SGU_CHUNK, SGU_HEADS, SGU_HEAD_DIM)
    mu = jnp.mean(v32, axis=-1, keepdims=True)
    var = jnp.mean(jnp.square(v32 - mu), axis=-1, keepdims=True)
    vn = (v32 - mu) * lax.rsqrt(var + EPS)
    tri = jnp.tril(jnp.ones((SGU_CHUNK, SGU_CHUNK), jnp.float32))
    ws = w_s.astype(jnp.float32) * tri[None]
    mixed = jnp.einsum('hts,bnshc->bnthc', ws, vn) + b_s.astype(jnp.float32).T[None, None, :, :, None]
    return u32 * mixed.reshape(B, S, GROUP_W)


def rglru_mixer(xb, gb, conv_w, conv_b, wa, ba, wx, bx, lam):
    B, S, C = xb.shape
    xc = lax.conv_general_dilated(
        xb, conv_w[:, None, :], window_strides=(1,), padding=[(CONV_WIDTH - 1, 0)],
        dimension_numbers=('NWC', 'WIO', 'NWC'), feature_group_count=C) + conv_b
    xc32 = xc.astype(jnp.float32)
    xblk = xc32.reshape(B, S, LRU_BLOCKS, LRU_BLOCK_DIM)
    r = jax.nn.sigmoid(jnp.einsum('bsgc,gcd->bsgd', xblk, wa.astype(jnp.float32)).reshape(B, S, C)
                       + ba.astype(jnp.float32))
    i = jax.nn.sigmoid(jnp.einsum('bsgc,gcd->bsgd', xblk, wx.astype(jnp.float32)).reshape(B, S, C)
                       + bx.astype(jnp.float32))
    log_a = -LRU_C * r * jax.nn.softplus(-lam.astype(jnp.float32))
    a = jnp.exp(log_a)
    mult = jnp.sqrt(-jnp.expm1(2.0 * log_a))
    b_in = mult * (i * xc32)

    def combine(e1, e2):
        a1, b1 = e1
        a2, b2 = e2
        return a1 * a2, a2 * b1 + b2

    _, h = lax.associative_scan(combine, (a, b_in), axis=1)
    return h * jax.nn.gelu(gb.astype(jnp.float32))


def setup_inputs(seed: int = 0) -> dict:
    key = jax.random.key(seed)
    ks = iter(jax.random.split(key, 32))
    f32 = jnp.float32

    def nrm(shape, scale):
        return jax.random.normal(next(ks), shape, f32) * scale

    x = jax.random.normal(next(ks), (BATCH, SEQ, D_MODEL), f32)
    ffn1_norm = 1.0 + nrm((DEPTH, D_MODEL), 0.02)
    ffn1_w_in = nrm((DEPTH, D_MODEL, 2 * D_FF), D_MODEL ** -0.5)
    ffn1_w_out = nrm((DEPTH, D_FF, D_MODEL), D_FF ** -0.5)
    mix_norm = 1.0 + nrm((DEPTH, D_MODEL), 0.02)
    mix_w_in = nrm((DEPTH, D_MODEL, D_IN), D_MODEL ** -0.5)
    mix_w_out = nrm((DEPTH, D_MIX, D_MODEL), D_MIX ** -0.5)
    pool_w = nrm((DEPTH, POOL_GROUPS, POOL_CH, POOL_CH), POOL_CH ** -0.5)
    pool_scale = 1.0 + nrm((DEPTH, GROUP_W), 0.1)
    sgu_w = nrm((DEPTH, SGU_HEADS, SGU_CHUNK, SGU_CHUNK), SGU_CHUNK ** -0.5)
    sgu_b = 1.0 + nrm((DEPTH, SGU_HEADS, SGU_CHUNK), 0.02)
    conv_w = nrm((DEPTH, CONV_WIDTH, GROUP_W), CONV_WIDTH ** -0.5)
    conv_b = nrm((DEPTH, GROUP_W), 0.01)
    lru_wa = nrm((DEPTH, LRU_BLOCKS, LRU_BLOCK_DIM, LRU_BLOCK_DIM), LRU_BLOCK_DIM ** -0.5)
    lru_ba = nrm((DEPTH, GROUP_W), 0.01)
    lru_wx = nrm((DEPTH, LRU_BLOCKS, LRU_BLOCK_DIM, LRU_BLOCK_DIM), LRU_BLOCK_DIM ** -0.5)
    lru_bx = nrm((DEPTH, GROUP_W), 0.01)
    a_c = jax.random.uniform(next(ks), (DEPTH, GROUP_W), f32, 0.9, 0.999)
    s = a_c ** (1.0 / LRU_C)
    lru_lambda = jnp.log(s) - jnp.log1p(-s)
    ffn2_norm = 1.0 + nrm((DEPTH, D_MODEL), 0.02)
    ffn2_w_in = nrm((DEPTH, D_MODEL, 2 * D_FF), D_MODEL ** -0.5)
    ffn2_w_out = nrm((DEPTH, D_FF, D_MODEL), D_FF ** -0.5)
    final_norm = 1.0 + nrm((D_MODEL,), 0.02)
    return {
        "x": x, "ffn1_norm": ffn1_norm, "ffn1_w_in": ffn1_w_in, "ffn1_w_out": ffn1_w_out,
        "mix_norm": mix_norm, "mix_w_in": mix_w_in, "mix_w_out": mix_w_out,
        "pool_w": pool_w, "pool_scale": pool_scale, "sgu_w": sgu_w, "sgu_b": sgu_b,
        "conv_w": conv_w, "conv_b": conv_b, "lru_wa": lru_wa, "lru_ba": lru_ba,
        "lru_wx": lru_wx, "lru_bx": lru_bx, "lru_lambda": lru_lambda,
        "ffn2_norm": ffn2_norm, "ffn2_w_in": ffn2_w_in, "ffn2_w_out": ffn2_w_out,
        "final_norm": final_norm,
    }


def reference(x, ffn1_norm, ffn1_w_in, ffn1_w_out, mix_norm, mix_w_in, mix_w_out,
              pool_w, pool_scale, sgu_w, sgu_b, conv_w, conv_b, lru_wa, lru_ba,
              lru_wx, lru_bx, lru_lambda, ffn2_norm, ffn2_w_in, ffn2_w_out, final_norm):
    for l in range(DEPTH):
        x = x + 0.5 * swiglu_ffn(rms_norm(x, ffn1_norm[l]), ffn1_w_in[l], ffn1_w_out[l])
        h = rms_norm(x, mix_norm[l])
        p = h @ mix_w_in[l]
        y_pool = pool_mixer(p[..., OFF_POOL:OFF_Q], pool_w[l], pool_scale[l])
        y_sb = stick_breaking_attention(p[..., OFF_Q:OFF_K], p[..., OFF_K:OFF_V], p[..., OFF_V:OFF_SGU_U])
        y_sgu = spatial_gating(p[..., OFF_SGU_U:OFF_SGU_V], p[..., OFF_SGU_V:OFF_LRU_X], sgu_w[l], sgu_b[l])
        y_lru = rglru_mixer(p[..., OFF_LRU_X:OFF_LRU_G], p[..., OFF_LRU_G:D_IN], conv_w[l], conv_b[l],
                            lru_wa[l], lru_ba[l], lru_wx[l], lru_bx[l], lru_lambda[l])
        y = jnp.concatenate([y_pool, y_sb, y_sgu, y_lru], axis=-1).astype(x.dtype)
        x = x + y @ mix_w_out[l]
        x = x + 0.5 * swiglu_ffn(rms_norm(x, ffn2_norm[l]), ffn2_w_in[l], ffn2_w_out[l])
    return rms_norm(x, final_norm)
```

```python
import contextlib
import numpy as np
import concourse.bass as bass
import concourse.mybir as mybir
from concourse.bass_utils import run_bass_kernel_spmd

F32 = mybir.dt.float32
BF16 = mybir.dt.bfloat16
AF = mybir.ActivationFunctionType
ALU = mybir.AluOpType
AX = mybir.AxisListType

D = 1024
DFF = 2816
NFC = DFF // 128
TT = 512


EPS = 1e-6
GELU_C = 0.7978845608028654


class _Op:
    __slots__ = ("eng", "fn", "dma", "deps", "ms", "dsem", "dval", "idx", "has_dep", "sw", "gen", "swk")

    def __init__(self, eng, fn, dma):
        self.eng = eng
        self.fn = fn
        self.dma = dma
        self.deps = ()
        self.ms = 0
        self.dsem = None
        self.dval = 0
        self.has_dep = False
        self.sw = None
        self.gen = 0
        self.swk = 0


class Prog:
    ENGS = ("pe", "act", "dve", "pool", "sp")

    def __init__(self, nc, n_dma_sems=20, same_engine_sync=True):
        self.nc = nc
        self.ops = {e: [] for e in self.ENGS}
        self.last_w = {}
        self.readers = {}
        self.n_dma_sems = n_dma_sems
        self.same_engine_sync = same_engine_sync
        self.dma_rr = 0
        self.dma_last = [None] * n_dma_sems
        self.dma_cnt = [0] * n_dma_sems
        self.sw_gen = {}
        self.sw_count = 0

    def add(self, eng, fn, reads=(), writes=(), dma=False, sw=None):
        op = _Op(eng, fn, dma)
        op.idx = len(self.ops[eng])
        self.ops[eng].append(op)
        deps = set()
        for k in reads:
            w = self.last_w.get(k)
            if w is not None:
                deps.add(w)
        for k in writes:
            w = self.last_w.get(k)
            if w is not None:
                deps.add(w)
            rd = self.readers.get(k)
            if rd:
                deps.update(rd.values())
        for k in reads:
            self.readers.setdefault(k, {})[eng] = op
        for k in writes:
            self.last_w[k] = op
            self.readers[k] = {}
        if sw is not None:
            op.sw = sw
            op.gen = self.sw_gen.get(sw, 0)
            self.sw_gen[sw] = op.gen + 1
            self.sw_count += 1
            op.swk = self.sw_count
        elif dma:
            s = self.dma_rr
            self.dma_rr = (self.dma_rr + 1) % self.n_dma_sems
            prev = self.dma_last[s]
            if prev is not None:
                deps.add(prev)
            self.dma_cnt[s] += 1
            op.dsem = s
            op.dval = 16 * self.dma_cnt[s]
            self.dma_last[s] = op
        deps.discard(op)
        op.deps = deps
        for d in deps:
            d.has_dep = True
        return op

    def pe(self, fn, r=(), w=()):
        return self.add("pe", fn, r, w)

    def act(self, fn, r=(), w=()):
        return self.add("act", fn, r, w)

    def dve(self, fn, r=(), w=()):
        return self.add("dve", fn, r, w)

    def pool(self, fn, r=(), w=()):
        return self.add("pool", fn, r, w)

    def dma(self, q, fn, r=(), w=()):
        return self.add(q, fn, r, w, dma=True)

    def dma_sw(self, slot, fn, r=(), w=()):
        return self.add("pool", fn, r, w, dma=True, sw=slot)

    def emit(self):
        nc = self.nc
        for e in self.ENGS:
            c = 0
            for op in self.ops[e]:
                if op.has_dep and not op.dma:
                    c += 1
                    op.ms = c
        with contextlib.ExitStack() as st:
            esem = {e: st.enter_context(nc.semaphore("ms_" + e)) for e in self.ENGS}
            dsem = [st.enter_context(nc.semaphore("dq%d" % i)) for i in range(self.n_dma_sems)]
            sw_unique = self.sw_count <= 100
            if sw_unique:
                swsem = {k: st.enter_context(nc.semaphore("sw%d" % k)) for k in range(1, self.sw_count + 1)}
            else:
                swsem = {k: st.enter_context(nc.semaphore("sw%s" % k)) for k in sorted(self.sw_gen)}
            block = st.enter_context(nc.Block())
            engobj = {"pe": nc.tensor, "act": nc.scalar, "dve": nc.vector,
                      "pool": nc.gpsimd, "sp": nc.sync}
            self.n_waits = 0

            def body(e):
                eng = engobj[e]
                waited = {}
                for op in self.ops[e]:
                    need = {}
                    for d in op.deps:
                        if d.sw is not None:
                            if sw_unique:
                                if not waited.get(("swu", d.swk)):
                                    waited[("swu", d.swk)] = 1
                                    eng.wait_ge(swsem[d.swk], 16)
                                    self.n_waits += 1
                            elif waited.get(("sw", d.sw), 0) < d.gen + 1:
                                waited[("sw", d.sw)] = d.gen + 1
                                eng.wait_ge(swsem[d.sw], 16 * (d.gen + 1))
                                self.n_waits += 1
                            continue
                        if d.dma:
                            key = ("d", d.dsem)
                            val = d.dval
                        else:
                            if d.eng == e and (e == "pe" or not self.same_engine_sync):
                                continue
                            key = ("e", d.eng)
                            val = d.ms
                        if need.get(key, 0) < val:
                            need[key] = val
                    for key, val in need.items():
                        if waited.get(key, 0) >= val:
                            continue
                        waited[key] = val
                        sem = dsem[key[1]] if key[0] == "d" else esem[key[1]]
                        eng.wait_ge(sem, val)
                        self.n_waits += 1
                    if op.sw is not None:
                        op.fn().then_inc(swsem[op.swk if sw_unique else op.sw], 16)
                        continue
                    inst = op.fn()
                    if op.dma:
                        inst.then_inc(dsem[op.dsem], 16)
                    elif op.ms:
                        inst.then_inc(esem[e], 1)

            @block.tensor
            def _(t):
                body("pe")

            @block.scalar
            def _(t):
                body("act")

            @block.vector
            def _(t):
                body("dve")

            @block.gpsimd
            def _(t):
                body("pool")

            @block.sync
            def _(t):
                body("sp")


class WStream:
    def __init__(self, tag, ch, nslot):
        self.tag = tag
        self.CH = ch
        self.NSLOT = nslot
        self.phases = {}
        self.total_chunks = 0
        self.dry = True
        self.cur_key = None
        self.pos = 0

    def begin_phase(self, key):
        self.cur_key = key
        self.pos = 0
        if self.dry:
            self.recording = key not in self.phases
            if self.recording:
                self.phases[key] = {"blocks": [], "nch": 0, "base": self.total_chunks}
        else:
            ph = self.phases[key]
            assert self.order[self.oi] == key, (self.order[self.oi], key)
            self.gbase = self.obase[self.oi]
            self.oi += 1

    def end_phase(self):
        if self.dry and self.recording:
            ph = self.phases[self.cur_key]
            ph["nch"] = (self.pos + self.CH - 1) // self.CH
            self.total_chunks += ph["nch"]

    def arm(self, p, nc, wst, ring, order):
        self.dry = False
        self.p, self.nc, self.wst, self.ring = p, nc, wst, ring
        self.order = order
        self.oi = 0
        self.G = []
        self.obase = []
        for key in order:
            ph = self.phases[key]
            self.obase.append(len(self.G))
            self.G.extend(range(ph["base"], ph["base"] + ph["nch"]))
        self.cur = -1
        self.issued = -1

    def _issue(self, g):
        if g >= len(self.G) or g <= self.issued:
            return
        assert g == self.issued + 1
        self.issued = g
        nc, ring, wst = self.nc, self.ring, self.wst
        slot = g % self.NSLOT
        c = self.G[g]
        a = self.CH // 1024
        self.p.dma_sw(self.tag + str(slot),
                      lambda: nc.gpsimd.dma_start(
                          out=ring[slot][:, :].rearrange("p (a b) -> p a b", a=a),
                          in_=wst[c].rearrange("p (a b) -> p a b", a=a)),
                      w=[("ring" + self.tag, slot)])

    def get(self, name, l, r0, c0, ncols=128, reserve=0):
        CH = self.CH
        if (self.pos % CH) + max(ncols, reserve) > CH:
            self.pos = (self.pos // CH + 1) * CH
        if self.dry:
            if self.recording:
                self.phases[self.cur_key]["blocks"].append((name, l, r0, c0, ncols, self.pos))
            self.pos += ncols
            return None, ("ring" + self.tag, 0)
        chunk = self.pos // CH
        off = self.pos % CH
        self.pos += ncols
        g = self.gbase + chunk
        if g != self.cur:
            if self.cur < 0:
                for i in range(self.NSLOT):
                    self._issue(i)
            else:
                assert g == self.cur + 1, (g, self.cur)
                self._issue(g - 1 + self.NSLOT)
            self.cur = g
        slot = g % self.NSLOT
        return self.ring[slot][:, off:off + ncols], ("ring" + self.tag, slot)


def _small_layout(L, final):
    off = {}
    n = 0

    def put(name, w):
        nonlocal n
        off[name] = n
        n += w
    for l in range(L):
        put(("g1", l), 8)
        put(("gm", l), 8)
        put(("g2", l), 8)
        put(("pscale", l), 2)
        put(("convw", l), 8)
        put(("convb", l), 2)
        put(("ba", l), 2)
        put(("bx", l), 2)
        put(("lam", l), 2)
    if final:
        put(("gf", 0), 8)
    return off, n


def _vec2(v):
    return np.ascontiguousarray(v.reshape(2, 128).T)


def _vec8(v):
    return np.ascontiguousarray(v.reshape(8, 128).T)


def _blockdiag(w2):
    o = np.zeros((128, 128), np.float32)
    o[0:64, 0:64] = w2[0]
    o[64:128, 64:128] = w2[1]
    return o


def _consts():
    j = np.arange(128)[:, None]
    s = np.arange(128)[None, :]
    ones = np.ones((128, 128), np.float32)
    onesneg = -ones
    trineg = -(j >= s).astype(np.float32)
    ident = np.eye(128, dtype=np.float32)
    negmask = np.where(j >= s, -30000.0, 0.0).astype(np.float32)
    return np.concatenate([ones, onesneg, trineg, ident, negmask], axis=1)


CHF, NSF = 3072, 3
CHM, NSM = 2048, 3
FGROUPS = ((0, 8), (8, 15), (15, 22))


def _merge(gens):
    vt = [0.0] * len(gens)
    alive = [True] * len(gens)
    while any(alive):
        i = min((j for j in range(len(gens)) if alive[j]), key=lambda j: vt[j])
        try:
            c = next(gens[i])
            vt[i] += (c or 0.1)
        except StopIteration:
            alive[i] = False


def _chain(*gens):
    for g in gens:
        yield from g


class _Z:
    def __getitem__(self, k):
        return self


def _schedule(NT, L):
    if NT < 2 or NT % 2:
        steps = []
        for t in range(NT):
            steps.append([[("F", t, 0, 1)]])
            for l in range(L):
                steps.append([[("M", t, l)]])
                nxt = [("F", t, l, 2)]
                nxt.append(("F", t, l + 1, 1) if l + 1 < L else ("FIN", t))
                steps.append([nxt])
            if t + 2 < NT:
                steps.append([[("LD", t + 2)]])
        return steps
    steps = [[[("F", 0, 0, 1)]]]
    carry = []
    for a in range(0, NT, 2):
        b, c, d = a + 1, a + 2, a + 3
        for l in range(L):
            if l == 0:
                fa = carry + [("F", b, 0, 1)]
            else:
                fa = [("F", b, l - 1, 2), ("F", b, l, 1)]
            steps.append([[("M", a, l)], fa])
            fb = [("F", a, l, 2)]
            if l + 1 < L:
                fb.append(("F", a, l + 1, 1))
            else:
                fb.append(("FIN", a))
                if c < NT:
                    fb += [("LD", c), ("F", c, 0, 1)]
            steps.append([[("M", b, l)], fb])
        carry = [("F", b, L - 1, 2), ("FIN", b)]
        if d < NT:
            carry.append(("LD", d))
    steps.append([carry])
    return steps


def build(S, L, final):
    NT = S // TT
    NKB = S // 128
    soff, NS = _small_layout(L, final)
    steps = _schedule(NT, L)

    wsF = WStream("F", CHF, NSF)
    wsM = WStream("M", CHM, NSM)
    G = _Gen(None, Prog(None), wsF, wsM, L, final, S, None, soff)
    for l in range(L):
        for which in (1, 2):
            for _ in G.g_ffn(0, l, which):
                pass
        for _ in G.g_mixer(0, l):
            pass
    orderF, orderM = [], []
    for st in steps:
        for stream in st:
            for ph in stream:
                if ph[0] == "F":
                    orderF.append(("F", ph[2], ph[3]))
                elif ph[0] == "M":
                    orderM.append(("M", ph[2]))

    nc = bass.Bass("TRN2", target_bir_lowering=False)
    x_in = nc.dram_tensor("xT", [D, S], F32, kind="ExternalInput").ap()
    wstF = nc.dram_tensor("wstF", [wsF.total_chunks, 128, CHF], F32, kind="ExternalInput").ap()
    wstM = nc.dram_tensor("wstM", [wsM.total_chunks, 128, CHM], F32, kind="ExternalInput").ap()
    small_in = nc.dram_tensor("small", [128, NS], F32, kind="ExternalInput").ap()
    cst_in = nc.dram_tensor("cst", [128, 5 * 128], F32, kind="ExternalInput").ap()
    bd_in = nc.dram_tensor("bd", [L, 6, 128, 128], F32, kind="ExternalInput").ap()
    ws_in = nc.dram_tensor("wsT", [L, 4, 128, 128], F32, kind="ExternalInput").ap()
    bs_in = nc.dram_tensor("bsb", [L, 2, 128, 128], F32, kind="ExternalInput").ap()
    y_out = nc.dram_tensor("yT", [D, S], F32, kind="ExternalOutput").ap()

    with contextlib.ExitStack() as st:
        def sb(name, shape, dt):
            return st.enter_context(nc.sbuf_tensor(name, shape, dt))

        T = {}
        T["xT"] = [sb("xT_sb%d" % i, [128, 8, TT], F32) for i in range(2)]
        T["xnF"] = sb("xnF", [128, 8, TT], BF16)
        T["aT"] = sb("aT", [128, 8, TT], BF16)
        T["Fs"] = [sb("Fs%d" % i, [128, TT], F32) for i in range(2)]
        T["xnM"] = sb("xnM", [128, 8, TT], BF16)
        T["yT"] = sb("yTm", [128, 8, TT], BF16)
        T["kc"] = sb("kc", [128, L * 2, S], BF16)
        T["vc"] = sb("vc", [128, L * NKB, 256], BF16)
        T["ringF"] = [sb("ringF%d" % i, [128, CHF], BF16) for i in range(NSF)]
        T["ringM"] = [sb("ringM%d" % i, [128, CHM], BF16) for i in range(NSM)]
        T["small"] = sb("small_sb", [128, NS], F32)
        T["cst"] = sb("cstb", [128, 5 * 128], BF16)
        T["bd"] = sb("bdb", [128, L * 6, 128], BF16)
        T["wsT"] = sb("wsTb", [128, L * 4, 128], BF16)
        T["bsb"] = sb("bsb_sb", [128, L * 2, 128], F32)
        T["cneg"] = sb("cneg", [128, L * 4], F32)
        T["nb"] = sb("nb", [128, L * 4], F32)
        T["epsb"] = sb("epsb", [128, 1], F32)
        T["icnt"] = sb("icnt", [128, 2, 16], F32)
        T["poolh"] = sb("poolh", [128, L * 2, 16], F32)
        T["lruh"] = sb("lruh", [128, L * 2, 3], F32)
        T["hst"] = sb("hst", [128, L * 2], F32)
        T["F"] = [sb("F%d" % i, [128, TT + 16], F32) for i in range(7)]
        T["B"] = [sb("B%d" % i, [128, TT], BF16) for i in range(6)]
        T["S"] = [sb("S%d" % i, [128, TT], BF16) for i in range(4)]
        T["E"] = [sb("E%d" % i, [128, TT], F32) for i in range(2)]
        T["qT"] = sb("qT", [128, 2, TT], BF16)
        T["st"] = sb("stat", [128, 32], F32)
        T["PS"] = [st.enter_context(nc.psum_tensor("ps%d" % i, [128, 512], F32)) for i in range(8)]
        T["x_in"] = x_in
        T["y_out"] = y_out

        p = Prog(nc)
        wsF.arm(p, nc, wstF, T["ringF"], orderF)
        wsM.arm(p, nc, wstM, T["ringM"], orderM)
        F = T["F"]

        p.dma("sp", lambda: nc.sync.dma_start(out=T["small"][:, :], in_=small_in[:, :]), w=["small"])
        for i in range(5):
            stg = F[i % 2]
            p.dma("sp", lambda i=i, stg=stg: nc.sync.dma_start(out=stg[:, 0:128], in_=cst_in[:, i * 128:(i + 1) * 128]),
                  w=[("F", i % 2)])
            p.dve(lambda i=i, stg=stg: nc.vector.tensor_copy(out=T["cst"][:, i * 128:(i + 1) * 128], in_=stg[:, 0:128]),
                  r=[("F", i % 2)], w=["cst"])
        k_ = 0
        for l in range(L):
            for i in range(6):
                stg = F[k_ % 2]
                p.dma("sp", lambda l=l, i=i, stg=stg: nc.sync.dma_start(out=stg[:, 0:128], in_=bd_in[l, i]),
                      w=[("F", k_ % 2)])
                p.dve(lambda l=l, i=i, stg=stg: nc.vector.tensor_copy(out=T["bd"][:, l * 6 + i, :], in_=stg[:, 0:128]),
                      r=[("F", k_ % 2)], w=["bd"])
                k_ += 1
            for h in range(4):
                stg = F[k_ % 2]
                p.dma("sp", lambda l=l, h=h, stg=stg: nc.sync.dma_start(out=stg[:, 0:128], in_=ws_in[l, h]),
                      w=[("F", k_ % 2)])
                p.pool(lambda l=l, h=h, stg=stg: nc.gpsimd.affine_select(
                    out=T["wsT"][:, l * 4 + h, :], in_=stg[:, 0:128], pattern=[[1, 128]],
                    compare_op=ALU.is_ge, fill=0.0, base=0, channel_multiplier=-1),
                    r=[("F", k_ % 2)], w=["wsT"])
                k_ += 1
            for c in range(2):
                p.dma("sp", lambda l=l, c=c: nc.sync.dma_start(out=T["bsb"][:, l * 2 + c, :], in_=bs_in[l, c]),
                      w=["bsb"])
            lo = soff[("lam", l)]
            p.act(lambda l=l, lo=lo: nc.scalar.activation(out=T["st"][:, 0:2], in_=T["small"][:, lo:lo + 2],
                                                           func=AF.Exp, scale=-1.0),
                  r=["small"], w=["st"])
            p.act(lambda l=l: nc.scalar.activation(out=T["st"][:, 2:4], in_=T["st"][:, 0:2], func=AF.Ln,
                                                    bias=1.0, scale=1.0),
                  r=["st"], w=["st"])
            p.dve(lambda l=l: nc.vector.tensor_scalar(out=T["cneg"][:, 2 * l:2 * l + 2], in0=T["st"][:, 2:4],
                                                      scalar1=-8.0, scalar2=None, op0=ALU.mult),
                  r=["st"], w=["cneg"])
            p.dve(lambda l=l: nc.vector.tensor_scalar(out=T["cneg"][:, 2 * L + 2 * l:2 * L + 2 * l + 2],
                                                      in0=T["st"][:, 2:4], scalar1=8.0, scalar2=None, op0=ALU.mult),
                  r=["st"], w=["cneg"])
        p.dve(lambda: nc.vector.memset(T["epsb"][:, :], EPS), w=["epsb"])
        for l in range(L):
            for k2_, nm in enumerate(("ba", "bx")):
                so = soff[(nm, l)]
                p.dve(lambda l=l, k2_=k2_, so=so: nc.vector.tensor_scalar(
                    out=T["nb"][:, l * 4 + 2 * k2_:l * 4 + 2 * k2_ + 2], in0=T["small"][:, so:so + 2],
                    scalar1=-1.0, scalar2=None, op0=ALU.mult), r=["small"], w=["nb"])
        wins = (2, 4, 8, 16)
        for c in range(2):
            for hf in range(2):
                g = 2 * c + hf
                p.dve(lambda c=c, hf=hf, g=g: nc.vector.memset(T["icnt"][64 * hf:64 * hf + 64, c, :], 1.0 / wins[g]),
                      w=["icnt"])
        for t in range(15):
            for c in range(2):
                for hf in range(2):
                    g = 2 * c + hf
                    if wins[g] > t + 1:
                        p.dve(lambda c=c, hf=hf, t=t: nc.vector.memset(
                            T["icnt"][64 * hf:64 * hf + 64, c, t:t + 1], 1.0 / (t + 1)), w=["icnt"])
        p.dve(lambda: nc.vector.memset(T["poolh"][:, :, :], 0.0), w=["poolh"])
        p.dve(lambda: nc.vector.memset(T["lruh"][:, :, :], 0.0), w=["lruh"])
        p.dve(lambda: nc.vector.memset(T["hst"][:, :], 0.0), w=["hst"])

        G = _Gen(nc, p, wsF, wsM, L, final, S, T, soff)
        G.load_x(0)
        if NT > 1:
            G.load_x(1)

        def _ld(t_):
            G.load_x(t_)
            yield 0.1
        for stp in steps:
            gens = []
            for stream in stp:
                parts = []
                for ph in stream:
                    if ph[0] == "F":
                        parts.append(G.g_ffn(ph[1], ph[2], ph[3]))
                    elif ph[0] == "M":
                        parts.append(G.g_mixer(ph[1], ph[2]))
                    elif ph[0] == "LD":
                        parts.append(_ld(ph[1]))
                    else:
                        parts.append(G.g_final(ph[1]))
                gens.append(_chain(*parts))
            _merge(gens)
        p.add("sp", lambda: nc.sync.nop(), reads=["y_hbm"])
        p.emit()
    packs = {"F": wsF, "M": wsM}
    return nc, packs, NS, soff


class _Gen:
    def __init__(self, nc, p, wsF, wsM, L, final, S, T, soff):
        self.nc, self.p, self.wsF, self.wsM = nc, p, wsF, wsM
        self.L, self.final, self.S, self.soff = L, final, S, soff
        if T is None:
            T = {k: _Z() for k in ("xT", "xnF", "xnM", "aT", "yT", "kc", "vc", "small", "cst", "bd", "wsT", "bsb",
                                   "cneg", "nb", "icnt", "poolh", "lruh", "hst", "qT", "st", "x_in", "y_out",
                                   "epsb")}
            T.update({"F": [None] * 7, "B": [None] * 6, "S": [None] * 4, "E": [None] * 2,
                      "PS": [None] * 8, "Fs": [None] * 2})
        self.T = T
        self.fbank = 0
        self.zrr = 0

    def load_x(self, t_):
        nc, p, T = self.nc, self.p, self.T
        buf = T["xT"][t_ % 2]
        p.dma("sp", lambda: nc.sync.dma_start(
            out=buf[:, :, :], in_=T["x_in"].rearrange("(c q) t -> q c t", q=128)[:, :, t_ * TT:t_ * TT + TT]),
            w=[("xT", t_ % 2, c) for c in range(8)])

    def _fb(self):
        v = (5, 6, 7)[self.fbank % 3]
        self.fbank += 1
        return v

    def rmsnorm(self, tt, gname, l, xn, xnk, bank, rbuf, rkey, out_fp32=False):
        nc, p, T = self.nc, self.p, self.T
        PS, small, cst = T["PS"], T["small"], T["cst"]
        xT = T["xT"][tt % 2]
        XP = tt % 2
        go = self.soff[(gname, l)]
        for c in range(8):
            if c % 2 == 0:
                p.act(lambda c=c: nc.scalar.activation(out=xn[:, c, :], in_=xT[:, c, :], func=AF.Square),
                      r=[("xT", XP, c)], w=[(xnk, c)])
            else:
                p.pool(lambda c=c: nc.gpsimd.tensor_tensor(out=xn[:, c, :], in0=xT[:, c, :], in1=xT[:, c, :],
                                                           op=ALU.mult),
                       r=[("xT", XP, c)], w=[(xnk, c)])
            p.pe(lambda c=c: nc.tensor.matmul(PS[bank][:, :], lhsT=cst[:, 0:128], rhs=xn[:, c, :],
                                              start=(c == 0), stop=(c == 7)),
                 r=[(xnk, c), "cst"], w=[("ps", bank)])
            yield 0.5
        p.act(lambda: nc.scalar.activation(out=rbuf[:, 0:TT], in_=PS[bank][:, :], func=AF.Ln,
                                           bias=T["epsb"][:, 0:1], scale=1.0 / D),
              r=[("ps", bank), "epsb"], w=[rkey])
        p.act(lambda: nc.scalar.activation(out=rbuf[:, 0:TT], in_=rbuf[:, 0:TT], func=AF.Exp, scale=-0.5),
              r=[rkey], w=[rkey])
        yield 1.0
        for c in range(8):
            dst = xT if out_fp32 else xn
            dk = ("xT", XP, c) if out_fp32 else (xnk, c)
            p.dve(lambda c=c, dst=dst: nc.vector.scalar_tensor_tensor(
                out=dst[:, c, :], in0=xT[:, c, :], scalar=small[:, go + c:go + c + 1], op0=ALU.mult,
                in1=rbuf[:, 0:TT], op1=ALU.mult),
                r=[("xT", XP, c), rkey, "small"], w=[dk])
            yield 0.5

    def g_ffn(self, tt, l, which):
        nc, p, T, ws = self.nc, self.p, self.T, self.wsF
        PS, xn, aT, Fs = T["PS"], T["xnF"], T["aT"], T["Fs"]
        xT = T["xT"][tt % 2]
        XP = tt % 2
        wi, wo = {1: ("ffn1_w_in", "ffn1_w_out"), 2: ("ffn2_w_in", "ffn2_w_out")}[which]
        ws.begin_phase(("F", l, which))
        yield from self.rmsnorm(tt, "g%d" % which, l, xn, "xnF", self._fb(), Fs[0], ("Fs", 0))
        for (f0, f1) in FGROUPS:
            for fc in range(f0, f1):
                j = fc - f0
                bg, bu = self._fb(), self._fb()
                for half, bank in ((0, bg), (1, bu)):
                    for kc in range(8):
                        wap, wk = ws.get(wi, l, kc * 128, half * DFF + fc * 128)
                        p.pe(lambda wap=wap, kc=kc, bank=bank: nc.tensor.matmul(
                            PS[bank][:, :], lhsT=wap, rhs=xn[:, kc, :], start=(kc == 0), stop=(kc == 7)),
                            r=[wk, ("xnF", kc)], w=[("ps", bank)])
                    yield 2.1
                f = Fs[j % 2]
                fk = ("Fs", j % 2)
                p.act(lambda bg=bg, f=f: nc.scalar.activation(out=f[:, :], in_=PS[bg][:, :], func=AF.Exp, scale=-1.0),
                      r=[("ps", bg)], w=[fk])
                p.act(lambda f=f: nc.scalar.activation(out=f[:, :], in_=f[:, :], func=AF.Ln, bias=1.0, scale=1.0),
                      r=[fk], w=[fk])
                p.act(lambda f=f: nc.scalar.activation(out=f[:, :], in_=f[:, :], func=AF.Exp, scale=-1.0),
                      r=[fk], w=[fk])
                p.dve(lambda bg=bg, f=f: nc.vector.tensor_tensor(out=f[:, :], in0=PS[bg][:, :], in1=f[:, :],
                                                                 op=ALU.mult),
                      r=[("ps", bg), fk], w=[fk])
                p.dve(lambda bu=bu, f=f, j=j: nc.vector.tensor_tensor(out=aT[:, j, :], in0=PS[bu][:, :],
                                                                       in1=f[:, :], op=ALU.mult),
                      r=[("ps", bu), fk], w=[("aT", j)])
            ng = f1 - f0
            for dc in range(8):
                bank = self._fb()
                for j in range(ng):
                    fc = f0 + j
                    wap, wk = ws.get(wo, l, fc * 128, dc * 128)
                    p.pe(lambda wap=wap, j=j, bank=bank, ng=ng: nc.tensor.matmul(
                        PS[bank][:, :], lhsT=wap, rhs=aT[:, j, :], start=(j == 0), stop=(j == ng - 1)),
                        r=[wk, ("aT", j)], w=[("ps", bank)])
                p.dve(lambda dc=dc, bank=bank: nc.vector.scalar_tensor_tensor(
                    out=xT[:, dc, :], in0=PS[bank][:, :], scalar=0.5, op0=ALU.mult, in1=xT[:, dc, :], op1=ALU.add),
                    r=[("ps", bank), ("xT", XP, dc)], w=[("xT", XP, dc)])
                yield 0.26 * ng
        ws.end_phase()

    def g_final(self, tt):
        nc, p, T = self.nc, self.p, self.T
        xT = T["xT"][tt % 2]
        XP = tt % 2
        t0 = tt * TT
        if self.final:
            yield from self.rmsnorm(tt, "gf", 0, T["xnF"], "xnF", self._fb(), T["Fs"][0], ("Fs", 0), out_fp32=True)
        p.dma("sp", lambda: nc.sync.dma_start(
            out=T["y_out"].rearrange("(c q) t -> q c t", q=128)[:, :, t0:t0 + TT], in_=xT[:, :, :]),
            r=[("xT", XP, c) for c in range(8)], w=["y_hbm"])
        yield 0.1

    def proj_fm(self, l, col0, bank=3):
        nc, p, T, ws = self.nc, self.p, self.T, self.wsM
        PS, xn = T["PS"], T["xnM"]
        for kc in range(8):
            wap, wk = ws.get("mix_w_in", l, kc * 128, col0)
            p.pe(lambda wap=wap, kc=kc: nc.tensor.matmul(
                PS[bank][:, :], lhsT=wap, rhs=xn[:, kc, :], start=(kc == 0), stop=(kc == 7)),
                r=[wk, ("xnM", kc)], w=[("ps", bank)])
        return bank

    def proj_tm_blocks(self, l, col0):
        return [self.wsM.get("mix_w_in", l, kc * 128, col0, 256, reserve=(8 - kc) * 256) for kc in range(8)]

    def proj_tm_tb(self, waps, tb, bank, half):
        nc, p, T = self.nc, self.p, self.T
        PS, xn = T["PS"], T["xnM"]
        for kc in range(8):
            wap, wk = waps[kc]
            p.pe(lambda wap=wap, kc=kc: nc.tensor.matmul(
                PS[bank][:, half * 256:half * 256 + 256], lhsT=xn[:, kc, tb * 128:(tb + 1) * 128], rhs=wap,
                start=(kc == 0), stop=(kc == 7), skip_group_check=True),
                r=[wk, ("xnM", kc)], w=[("ps", bank)])

    def gelu(self, dst_fn, dkeys, src_fn, skeys, ia, ib, width):
        nc, p, T = self.nc, self.p, self.T
        F = T["F"]
        fa, fb = F[ia], F[ib]
        ka, kb = ("F", ia), ("F", ib)
        w = width
        cst_ = w / 1000.0
        p.dve(lambda: nc.vector.tensor_copy(out=fa[:, 0:w], in_=src_fn()), r=skeys, w=[ka])
        yield cst_
        p.pool(lambda: nc.gpsimd.tensor_tensor(out=fb[:, 0:w], in0=fa[:, 0:w], in1=fa[:, 0:w], op=ALU.mult),
               r=[ka], w=[kb])
        yield cst_
        p.dve(lambda: nc.vector.tensor_scalar(out=fb[:, 0:w], in0=fb[:, 0:w], scalar1=0.044715, scalar2=1.0,
                                              op0=ALU.mult, op1=ALU.add), r=[kb], w=[kb])
        yield cst_
        p.pool(lambda: nc.gpsimd.tensor_tensor(out=fb[:, 0:w], in0=fb[:, 0:w], in1=fa[:, 0:w], op=ALU.mult),
               r=[ka, kb], w=[kb])
        yield cst_
        p.act(lambda: nc.scalar.activation(out=fb[:, 0:w], in_=fb[:, 0:w], func=AF.Exp, scale=-2.0 * GELU_C),
              r=[kb], w=[kb])
        yield cst_
        p.pool(lambda: nc.gpsimd.tensor_scalar(out=fb[:, 0:w], in0=fb[:, 0:w], scalar1=1.0, scalar2=1.0,
                                               op0=ALU.mult, op1=ALU.add), r=[kb], w=[kb])
        yield cst_
        p.dve(lambda: nc.vector.reciprocal(out=fb[:, 0:w], in_=fb[:, 0:w]), r=[kb], w=[kb])
        yield 3 * cst_
        p.pool(lambda: nc.gpsimd.tensor_tensor(out=dst_fn(), in0=fa[:, 0:w], in1=fb[:, 0:w], op=ALU.mult),
               r=[ka, kb], w=dkeys)
        yield cst_

    def g_mixer(self, tt, l):
        nc, p, T, ws = self.nc, self.p, self.T, self.wsM
        L, S, soff = self.L, self.S, self.soff
        NKB = S // 128
        t0 = tt * TT
        PS, xn, yT = T["PS"], T["xnM"], T["yT"]
        xT = T["xT"][tt % 2]
        XP = tt % 2
        small, cst = T["small"], T["cst"]
        F, B, E = T["F"], T["B"], T["E"]
        ONESNEG = lambda: cst[:, 128:256]
        TRINEG = lambda: cst[:, 256:384]
        IDENT = lambda: cst[:, 384:512]
        NEGMASK = lambda: cst[:, 512:640]
        C0 = {"pool": 0, "q": 256, "k": 512, "v": 768, "su": 1024, "sv": 1280, "lx": 1536, "lg": 1792}
        kc_t, vc_t, qT = T["kc"], T["vc"], T["qT"]
        wins = (2, 4, 8, 16)
        OB, NB_, ACC = 3, 4, 2
        gen = self

        ws.begin_phase(("M", l))
        yield from self.rmsnorm(tt, "gm", l, xn, "xnM", NB_, F[6], ("F", 6))

        for c in range(2):
            bank = self.proj_fm(l, C0["k"] + c * 128)
            p.dve(lambda bank=bank, c=c: nc.vector.tensor_copy(out=kc_t[:, l * 2 + c, t0:t0 + TT], in_=PS[bank][:, :]),
                  r=[("ps", bank)], w=[("kc", l, c)])
            yield 2.1
            bank = self.proj_fm(l, C0["q"] + c * 128)
            p.act(lambda bank=bank, c=c: nc.scalar.activation(out=qT[:, c, :], in_=PS[bank][:, :], func=AF.Copy,
                                                               scale=0.125),
                  r=[("ps", bank)], w=[("qT", c)])
            yield 2.1
        waps = self.proj_tm_blocks(l, C0["v"])
        for tb in range(4):
            half = tb % 2
            self.proj_tm_tb(waps, tb, OB, half)
            kb = tt * 4 + tb
            p.dve(lambda half=half, kb=kb: nc.vector.tensor_copy(
                out=vc_t[:, l * NKB + kb, :], in_=PS[OB][:, half * 256:half * 256 + 256]),
                r=[("ps", OB)], w=[("vc", l)])
            yield 1.1

        nkb = 4 * tt + 4

        def g_attn():
            for hp in range(2):
                hs = (2 * hp, 2 * hp + 1)
                ulist = [(kb, h) for kb in range(nkb - 1, -1, -1) for h in hs]
                state = {"ei": 0}

                def stage1(u):
                    kb, h = u
                    c, pb = h // 2, 64 * (h % 2)
                    j = kb - 4 * tt
                    c0 = max(0, 128 * j)
                    zb = gen.zrr % 2
                    gen.zrr += 1
                    ei = state["ei"]
                    state["ei"] = ei + 1
                    e2 = ei % 2
                    ef, spb, ab = E[e2], B[2 + e2], B[4 + e2]
                    state[u] = (zb, ef, spb, ab, e2, c0)
                    p.pe(lambda: nc.tensor.matmul(
                        PS[zb][:, c0:TT], lhsT=kc_t[pb:pb + 64, l * 2 + c, kb * 128:(kb + 1) * 128],
                        rhs=qT[pb:pb + 64, c, c0:TT], start=True, stop=False, skip_group_check=True),
                        r=[("kc", l, c), ("qT", c)], w=[("ps", zb)])
                    if j >= 0:
                        p.pe(lambda: nc.tensor.matmul(
                            PS[zb][:, c0:c0 + 128], lhsT=IDENT(), rhs=NEGMASK(), start=False, stop=False,
                            skip_group_check=True),
                            r=["cst"], w=[("ps", zb)])
                    p.act(lambda: nc.scalar.activation(out=ef[:, c0:TT], in_=PS[zb][:, c0:TT], func=AF.Exp),
                          r=[("ps", zb)], w=[("E", e2)])
                    p.act(lambda: nc.scalar.activation(out=spb[:, c0:TT], in_=ef[:, c0:TT], func=AF.Ln,
                                                       bias=1.0, scale=1.0),
                          r=[("E", e2)], w=[("B", 2 + e2)])

                def stage2(u):
                    kb, h = u
                    zb, ef, spb, ab, e2, c0 = state.pop(u)
                    j = kb - 4 * tt
                    first = (kb == nkb - 1)
                    Sh = T["S"][h]
                    cs = c0 + 128 if j >= 0 else 0
                    p.pe(lambda: nc.tensor.matmul(
                        PS[zb][:, c0:TT], lhsT=TRINEG(), rhs=spb[:, c0:TT], start=False, stop=first,
                        skip_group_check=True),
                        r=["cst", ("B", 2 + e2)], w=[("ps", zb)])
                    if not first:
                        p.pe(lambda: nc.tensor.matmul(
                            PS[zb][:, cs:TT], lhsT=ONESNEG(), rhs=Sh[:, cs:TT], start=False, stop=True,
                            skip_group_check=True),
                            r=["cst", ("S", h)], w=[("ps", zb)])
                    if kb > 0:
                        if first:
                            p.pool(lambda: nc.gpsimd.tensor_copy(out=Sh[:, c0:TT], in_=spb[:, c0:TT]),
                                   r=[("B", 2 + e2)], w=[("S", h)])
                        else:
                            p.pool(lambda: nc.gpsimd.tensor_tensor(out=Sh[:, cs:TT], in0=Sh[:, cs:TT],
                                                                   in1=spb[:, cs:TT], op=ALU.add),
                                   r=[("B", 2 + e2), ("S", h)], w=[("S", h)])
                            if j >= 0:
                                p.pool(lambda: nc.gpsimd.tensor_copy(out=Sh[:, c0:cs], in_=spb[:, c0:cs]),
                                       r=[("B", 2 + e2)], w=[("S", h)])
                    p.act(lambda: nc.scalar.activation(out=ab[:, c0:TT], in_=PS[zb][:, c0:TT], func=AF.Exp),
                          r=[("ps", zb)], w=[("B", 4 + e2)])
                    state[("s3", u)] = (ab, e2, c0)

                def stage3(u):
                    kb, h = u
                    c, pb = h // 2, 64 * (h % 2)
                    ab, e2, c0 = state.pop(("s3", u))
                    first = (kb == nkb - 1)
                    p.pe(lambda: nc.tensor.matmul(
                        PS[ACC][pb:pb + 64, c0:TT], lhsT=vc_t[:, l * NKB + kb, h * 64:(h + 1) * 64],
                        rhs=ab[:, c0:TT], start=first, stop=(kb == 0), skip_group_check=True),
                        r=[("vc", l), ("B", 4 + e2)], w=[("ps", ACC, h % 2)])
                    if kb == 0:
                        p.dve(lambda: nc.vector.tensor_copy(out=yT[pb:pb + 64, 2 + c, :],
                                                            in_=PS[ACC][pb:pb + 64, :]),
                              r=[("ps", ACC, h % 2)], w=[("yT", 2 + c)])

                n = len(ulist)
                for i in range(n + 2):
                    if i < n:
                        stage1(ulist[i])
                    if 1 <= i <= n:
                        stage2(ulist[i - 1])
                    if i >= 2:
                        stage3(ulist[i - 2])
                    yield 1.75

        def g_pool(c):
            X, S2, S4 = F[2], F[3], F[4]
            kX, k2, k4 = ("F", 2), ("F", 3), ("F", 4)
            hidx = l * 2 + c
            W = TT + 16
            p.dve(lambda: nc.vector.tensor_copy(out=X[:, 0:16], in_=T["poolh"][:, hidx, :]),
                  r=[("poolh", hidx)], w=[kX])
            self.proj_fm(l, C0["pool"] + c * 128)
            yield 2.1
            p.dve(lambda: nc.vector.tensor_copy(out=X[:, 16:16 + TT], in_=PS[OB][:, :]),
                  r=[("ps", OB)], w=[kX])
            yield 0.5
            p.dve(lambda: nc.vector.tensor_copy(out=T["poolh"][:, hidx, :], in_=X[:, TT:TT + 16]),
                  r=[kX], w=[("poolh", hidx)])
            p.pool(lambda: nc.gpsimd.tensor_tensor(out=S2[:, 1:W], in0=X[:, 1:W], in1=X[:, 0:W - 1], op=ALU.add),
                   r=[kX], w=[k2])
            yield 0.5
            p.pool(lambda: nc.gpsimd.tensor_tensor(out=S4[:, 3:W], in0=S2[:, 3:W], in1=S2[:, 1:W - 2], op=ALU.add),
                   r=[k2], w=[k4])
            yield 0.5
            if c == 0:
                lvl, lk = (S2, S4), (k2, k4)
            else:
                S8, S16 = F[5], F[3]
                k8, k16 = ("F", 5), ("F", 3)
                p.pool(lambda: nc.gpsimd.tensor_tensor(out=S8[:, 7:W], in0=S4[:, 7:W], in1=S4[:, 3:W - 4],
                                                       op=ALU.add), r=[k4], w=[k8])
                yield 0.5
                p.pool(lambda: nc.gpsimd.tensor_tensor(out=S16[:, 15:W], in0=S8[:, 15:W], in1=S8[:, 7:W - 8],
                                                       op=ALU.add), r=[k8], w=[k16])
                yield 0.5
                lvl, lk = (S8, S16), (k8, k16)
            db = B[0]
            for hf in range(2):
                g = 2 * c + hf
                sl = slice(64 * hf, 64 * hf + 64)
                Lv, kk = lvl[hf], lk[hf]
                p.dve(lambda Lv=Lv, sl=sl, g=g: nc.vector.scalar_tensor_tensor(
                    out=db[sl, :], in0=Lv[sl, 16:16 + TT], scalar=1.0 / wins[g], op0=ALU.mult,
                    in1=X[sl, 16:16 + TT], op1=ALU.subtract),
                    r=[kk, kX], w=[("B", 0)])
                if tt == 0:
                    p.dve(lambda Lv=Lv, sl=sl: nc.vector.tensor_tensor(
                        out=Lv[sl, 16:32], in0=Lv[sl, 16:32], in1=T["icnt"][sl, c, :], op=ALU.mult),
                        r=[kk, "icnt"], w=[kk])
                    p.dve(lambda Lv=Lv, sl=sl: nc.vector.tensor_tensor(
                        out=db[sl, 0:16], in0=Lv[sl, 16:32], in1=X[sl, 16:32], op=ALU.subtract),
                        r=[kk, kX], w=[("B", 0)])
                yield 0.5
            p.pe(lambda: nc.tensor.matmul(PS[OB][:, :], lhsT=T["bd"][:, l * 6 + c, :], rhs=db[:, :],
                                          start=True, stop=True),
                 r=["bd", ("B", 0)], w=[("ps", OB)])
            po = soff[("pscale", l)]
            yield 0.3
            p.dve(lambda: nc.vector.tensor_scalar(
                out=yT[:, c, :], in0=PS[OB][:, :], scalar1=small[:, po + c:po + c + 1], scalar2=None,
                op0=ALU.mult),
                r=[("ps", OB), "small"], w=[("yT", c)])
            yield 0.5

        def g_sgu():
            ug = [F[2], F[3]]
            for c in range(2):
                self.proj_fm(l, C0["su"] + c * 128)
                yield 2.1
                yield from self.gelu(lambda c=c: ug[c][:, 0:TT], [("F", 2 + c)], lambda: PS[OB][:, :],
                                     [("ps", OB)], 4, 5, TT)
            waps = self.proj_tm_blocks(l, C0["sv"])
            stt = T["st"]
            ks = "st"
            vg = F[4]
            kv = ("F", 4)
            vn = B[1]
            for tb in range(4):
                half = tb % 2
                self.proj_tm_tb(waps, tb, OB, half)
                yield 1.1
                yield from self.gelu(lambda: vg[:, 0:256], [kv],
                                     lambda half=half: PS[OB][:, half * 256:half * 256 + 256],
                                     [("ps", OB)], 0, 1, 256)
                p.dve(lambda: nc.vector.tensor_reduce(
                    out=stt[:, 8:12], in_=vg[:, 0:256].rearrange("p (h d) -> p h d", h=4), axis=AX.X, op=ALU.add),
                    r=[kv], w=[ks])
                p.pool(lambda: nc.gpsimd.tensor_tensor(out=vg[:, 256:512], in0=vg[:, 0:256], in1=vg[:, 0:256],
                                                       op=ALU.mult), r=[kv], w=[kv])
                yield 0.5
                p.dve(lambda: nc.vector.tensor_reduce(
                    out=stt[:, 12:16], in_=vg[:, 256:512].rearrange("p (h d) -> p h d", h=4), axis=AX.X,
                    op=ALU.add), r=[kv], w=[ks])
                p.dve(lambda: nc.vector.tensor_scalar(out=stt[:, 8:12], in0=stt[:, 8:12], scalar1=1.0 / 64,
                                                      scalar2=None, op0=ALU.mult), r=[ks], w=[ks])
                p.dve(lambda: nc.vector.tensor_tensor(out=stt[:, 16:20], in0=stt[:, 8:12], in1=stt[:, 8:12],
                                                      op=ALU.mult), r=[ks], w=[ks])
                p.dve(lambda: nc.vector.scalar_tensor_tensor(out=stt[:, 12:16], in0=stt[:, 12:16], scalar=1.0 / 64,
                                                             op0=ALU.mult, in1=stt[:, 16:20], op1=ALU.subtract),
                      r=[ks], w=[ks])
                yield 0.4
                p.act(lambda: nc.scalar.activation(out=stt[:, 12:16], in_=stt[:, 12:16], func=AF.Ln,
                                                   bias=T["epsb"][:, 0:1], scale=1.0), r=[ks, "epsb"], w=[ks])
                p.act(lambda: nc.scalar.activation(out=stt[:, 12:16], in_=stt[:, 12:16], func=AF.Exp, scale=-0.5),
                      r=[ks], w=[ks])
                yield 0.3
                for h in range(4):
                    p.dve(lambda h=h: nc.vector.tensor_scalar(
                        out=vn[:, h * 64:(h + 1) * 64], in0=vg[:, h * 64:(h + 1) * 64],
                        scalar1=stt[:, 8 + h:9 + h], scalar2=stt[:, 12 + h:13 + h], op0=ALU.subtract, op1=ALU.mult),
                        r=[kv, ks], w=[("B", 1)])
                yield 0.4
                rg = (tb % 2) * 256
                for h in range(4):
                    c, pb = h // 2, 64 * (h % 2)
                    p.pe(lambda h=h, c=c, pb=pb, rg=rg: nc.tensor.matmul(
                        PS[NB_][pb:pb + 64, rg + c * 128:rg + c * 128 + 128], lhsT=vn[:, h * 64:(h + 1) * 64],
                        rhs=T["wsT"][:, l * 4 + h, :], start=True, stop=True, skip_group_check=True),
                        r=[("B", 1), "wsT"], w=[("ps", NB_)])
                yield 0.3
                for c in range(2):
                    sl = slice(tb * 128, (tb + 1) * 128)
                    p.dve(lambda c=c, sl=sl, rg=rg: nc.vector.tensor_tensor(
                        out=F[5][:, c * 128:c * 128 + 128], in0=PS[NB_][:, rg + c * 128:rg + c * 128 + 128],
                        in1=T["bsb"][:, l * 2 + c, :], op=ALU.add),
                        r=[("ps", NB_), "bsb"], w=[("F", 5)])
                    p.pool(lambda c=c, sl=sl: nc.gpsimd.tensor_tensor(
                        out=yT[:, 4 + c, sl], in0=F[5][:, c * 128:c * 128 + 128], in1=ug[c][:, sl], op=ALU.mult),
                        r=[("F", 5), ("F", 2 + c)], w=[("yT", 4 + c)])
                yield 0.4

        def g_lru(c):
            XB, XC, R, I = F[2], F[3], F[4], F[5]
            kXB, kXC, kR, kI = ("F", 2), ("F", 3), ("F", 4), ("F", 5)
            A_, TH = F[0], F[1]
            kA, kTH = ("F", 0), ("F", 1)
            hidx = l * 2 + c
            p.dve(lambda: nc.vector.tensor_copy(out=XB[:, 0:3], in_=T["lruh"][:, hidx, :]),
                  r=[("lruh", hidx)], w=[kXB])
            self.proj_fm(l, C0["lx"] + c * 128)
            yield 2.1
            p.dve(lambda: nc.vector.tensor_copy(out=XB[:, 3:3 + TT], in_=PS[OB][:, :]),
                  r=[("ps", OB)], w=[kXB])
            yield 0.5
            p.dve(lambda: nc.vector.tensor_copy(out=T["lruh"][:, hidx, :], in_=XB[:, TT:TT + 3]),
                  r=[kXB], w=[("lruh", hidx)])
            cw = soff[("convw", l)] + c * 4
            cb = soff[("convb", l)] + c
            p.dve(lambda: nc.vector.tensor_scalar(out=XC[:, 0:TT], in0=XB[:, 3:3 + TT],
                                                  scalar1=small[:, cw + 3:cw + 4], scalar2=small[:, cb:cb + 1],
                                                  op0=ALU.mult, op1=ALU.add),
                  r=[kXB, "small"], w=[kXC])
            yield 0.5
            for j in range(3):
                p.dve(lambda j=j: nc.vector.scalar_tensor_tensor(
                    out=XC[:, 0:TT], in0=XB[:, j:j + TT], scalar=small[:, cw + j:cw + j + 1], op0=ALU.mult,
                    in1=XC[:, 0:TT], op1=ALU.add),
                    r=[kXB, kXC, "small"], w=[kXC])
                yield 0.5
            xcb = B[0]
            p.pool(lambda: nc.gpsimd.tensor_copy(out=xcb[:, :], in_=XC[:, 0:TT]), r=[kXC], w=[("B", 0)])
            yield 0.5
            nbo = l * 4 + c
            for (dst, kd, wi_, bo) in ((R, kR, 2, nbo), (I, kI, 4, nbo + 2)):
                p.pe(lambda wi_=wi_: nc.tensor.matmul(
                    PS[OB][:, :], lhsT=T["bd"][:, l * 6 + wi_ + c, :], rhs=xcb[:, :], start=True, stop=True),
                    r=["bd", ("B", 0)], w=[("ps", OB)])
                yield 0.3
                p.act(lambda dst=dst, bo=bo: nc.scalar.activation(
                    out=dst[:, 0:TT], in_=PS[OB][:, :], func=AF.Exp, bias=T["nb"][:, bo:bo + 1], scale=-1.0),
                    r=[("ps", OB), "nb"], w=[kd])
                yield 0.5
                p.pool(lambda dst=dst: nc.gpsimd.tensor_scalar(out=dst[:, 0:TT], in0=dst[:, 0:TT], scalar1=1.0,
                                                               scalar2=1.0, op0=ALU.mult, op1=ALU.add),
                       r=[kd], w=[kd])
                yield 0.5
                p.dve(lambda dst=dst: nc.vector.reciprocal(out=dst[:, 0:TT], in_=dst[:, 0:TT]), r=[kd], w=[kd])
                yield 1.5
            p.act(lambda: nc.scalar.activation(out=A_[:, 0:TT], in_=R[:, 0:TT], func=AF.Exp,
                                               scale=T["cneg"][:, l * 2 + c:l * 2 + c + 1]),
                  r=[kR, "cneg"], w=[kA])
            p.act(lambda: nc.scalar.activation(out=TH[:, 0:TT], in_=R[:, 0:TT], func=AF.Tanh,
                                               scale=T["cneg"][:, 2 * L + l * 2 + c:2 * L + l * 2 + c + 1]),
                  r=[kR, "cneg"], w=[kTH])
            yield 1.0
            p.pool(lambda: nc.gpsimd.tensor_tensor(out=I[:, 0:TT], in0=I[:, 0:TT], in1=XC[:, 0:TT], op=ALU.mult),
                   r=[kI, kXC], w=[kI])
            yield 0.5
            p.pool(lambda: nc.gpsimd.tensor_tensor(out=R[:, 0:TT], in0=A_[:, 0:TT], in1=A_[:, 0:TT], op=ALU.mult),
                   r=[kA, kTH], w=[kR])
            yield 0.5
            p.dve(lambda: nc.vector.scalar_tensor_tensor(out=R[:, 0:TT], in0=R[:, 0:TT], scalar=1.0, op0=ALU.add,
                                                         in1=TH[:, 0:TT], op1=ALU.mult),
                  r=[kR, kTH], w=[kR])
            p.dve(lambda: nc.vector.tensor_scalar(out=R[:, 0:TT], in0=R[:, 0:TT], scalar1=1e-30, scalar2=None,
                                                  op0=ALU.max), r=[kR], w=[kR])
            yield 1.0
            p.act(lambda: nc.scalar.activation(out=R[:, 0:TT], in_=R[:, 0:TT], func=AF.Ln), r=[kR], w=[kR])
            p.act(lambda: nc.scalar.activation(out=R[:, 0:TT], in_=R[:, 0:TT], func=AF.Exp, scale=0.5),
                  r=[kR], w=[kR])
            yield 1.0
            p.dve(lambda: nc.vector.tensor_tensor(out=I[:, 0:TT], in0=I[:, 0:TT], in1=R[:, 0:TT], op=ALU.mult),
                  r=[kI, kR], w=[kI])
            yield 0.5
            p.dve(lambda: nc.vector.tensor_tensor_scan(
                out=XC[:, 0:TT], data0=A_[:, 0:TT], data1=I[:, 0:TT], initial=T["hst"][:, hidx:hidx + 1],
                op0=ALU.mult, op1=ALU.add),
                r=[kA, kI, ("hst", hidx)], w=[kXC])
            p.dve(lambda: nc.vector.tensor_copy(out=T["hst"][:, hidx:hidx + 1], in_=XC[:, TT - 1:TT]),
                  r=[kXC], w=[("hst", hidx)])
            yield 1.0
            self.proj_fm(l, C0["lg"] + c * 128)
            yield 2.1
            yield from self.gelu(lambda: F[4][:, 0:TT], [("F", 4)], lambda: PS[OB][:, :], [("ps", OB)], 0, 1, TT)
            p.dve(lambda: nc.vector.tensor_tensor(out=yT[:, 6 + c, :], in0=XC[:, 0:TT], in1=F[4][:, 0:TT],
                                                  op=ALU.mult),
                  r=[kXC, ("F", 4)], w=[("yT", 6 + c)])
            yield 0.5

        def g_acd():
            yield from g_pool(0)
            yield from g_pool(1)
            yield from g_sgu()
            yield from g_lru(0)
            yield from g_lru(1)

        ga, gb = g_attn(), g_acd()
        va = vb = 0.0
        alive_a = alive_b = True
        while alive_a or alive_b:
            if alive_a and (va <= vb or not alive_b):
                try:
                    c_ = next(ga)
                    va += c_
                    if va > vb or not alive_b:
                        yield c_
                except StopIteration:
                    alive_a = False
            else:
                try:
                    c_ = next(gb)
                    vb += c_
                    if vb > va or not alive_a:
                        yield c_
                except StopIteration:
                    alive_b = False

        for dc in range(8):
            for kc in range(8):
                wap, wk = ws.get("mix_w_out", l, kc * 128, dc * 128)
                p.pe(lambda wap=wap, kc=kc: nc.tensor.matmul(
                    PS[OB][:, :], lhsT=wap, rhs=yT[:, kc, :], start=(kc == 0), stop=(kc == 7)),
                    r=[wk, ("yT", kc)], w=[("ps", OB)])
            p.dve(lambda dc=dc: nc.vector.tensor_tensor(
                out=xT[:, dc, :], in0=PS[OB][:, :], in1=xT[:, dc, :], op=ALU.add),
                r=[("ps", OB), ("xT", XP, dc)], w=[("xT", XP, dc)])
            yield 2.1
        ws.end_phase()


_CACHE = {}


def _get_prog(S, L, final):
    key = (S, L, final)
    if key not in _CACHE:
        _CACHE[key] = build(S, L, final)
    return _CACHE[key]


def _pack_weights(ws, W, lsel):
    CH = ws.CH
    wst = np.zeros((ws.total_chunks, 128, CH), np.float32)
    for key, ph in ws.phases.items():
        base = ph["base"]
        for (name, l, r0, c0, ncols, pos) in ph["blocks"]:
            ch, off = base + pos // CH, pos % CH
            wst[ch, :, off:off + ncols] = W[name][lsel[l], r0:r0 + 128, c0:c0 + ncols]
    return wst


def _run(xT_list, W, lsel, final):
    S = xT_list[0].shape[1]
    L = len(lsel)
    nc, packs, NS, soff = _get_prog(S, L, final)
    wstF = _pack_weights(packs["F"], W, lsel)
    wstM = _pack_weights(packs["M"], W, lsel)
    small = np.zeros((128, NS), np.float32)
    for li, l in enumerate(lsel):
        small[:, soff[("g1", li)]:soff[("g1", li)] + 8] = _vec8(W["ffn1_norm"][l])
        small[:, soff[("gm", li)]:soff[("gm", li)] + 8] = _vec8(W["mix_norm"][l])
        small[:, soff[("g2", li)]:soff[("g2", li)] + 8] = _vec8(W["ffn2_norm"][l])
        small[:, soff[("pscale", li)]:soff[("pscale", li)] + 2] = _vec2(W["pool_scale"][l])
        cw = W["conv_w"][l]
        for c in range(2):
            for j in range(4):
                small[:, soff[("convw", li)] + c * 4 + j] = cw[j, c * 128:(c + 1) * 128]
        small[:, soff[("convb", li)]:soff[("convb", li)] + 2] = _vec2(W["conv_b"][l])
        small[:, soff[("ba", li)]:soff[("ba", li)] + 2] = _vec2(W["lru_ba"][l])
        small[:, soff[("bx", li)]:soff[("bx", li)] + 2] = _vec2(W["lru_bx"][l])
        small[:, soff[("lam", li)]:soff[("lam", li)] + 2] = _vec2(W["lru_lambda"][l])
    if final:
        small[:, soff[("gf", 0)]:soff[("gf", 0)] + 8] = _vec8(W["final_norm"])
    bd = np.zeros((L, 6, 128, 128), np.float32)
    wsT = np.zeros((L, 4, 128, 128), np.float32)
    bsb = np.zeros((L, 2, 128, 128), np.float32)
    for li, l in enumerate(lsel):
        for c in range(2):
            bd[li, c] = _blockdiag(W["pool_w"][l, 2 * c:2 * c + 2])
            bd[li, 2 + c] = _blockdiag(W["lru_wa"][l, 2 * c:2 * c + 2])
            bd[li, 4 + c] = _blockdiag(W["lru_wx"][l, 2 * c:2 * c + 2])
            for hf in range(2):
                bsb[li, c, 64 * hf:64 * hf + 64, :] = W["sgu_b"][l, 2 * c + hf][None, :]
        for h in range(4):
            wsT[li, h] = W["sgu_w"][l, h].T
    cst = _consts()
    shared = {"wstF": wstF, "wstM": wstM, "small": small, "cst": cst, "bd": bd, "wsT": wsT, "bsb": bsb}
    in_maps = [dict(shared, xT=np.ascontiguousarray(x)) for x in xT_list]
    n = len(xT_list)
    res = run_bass_kernel_spmd(nc, in_maps, core_ids=list(range(n)))
    return [r["yT"] for r in res.results]


def kernel(**inputs):
    W = {k: np.asarray(v, dtype=np.float32) for k, v in inputs.items()}
    x = W.pop("x")
    Bn = x.shape[0]
    xT = [np.ascontiguousarray(x[b].T) for b in range(Bn)]
    depth = W["ffn1_norm"].shape[0]
    outs = _run(xT, W, list(range(depth)), True)
    return np.stack([o.T for o in outs], axis=0).astype(np.float32)
```

```python
import contextlib
import numpy as np
import concourse.bass as bass
import concourse.mybir as mybir
from concourse.bass_utils import run_bass_kernel_spmd

F32 = mybir.dt.float32
BF16 = mybir.dt.bfloat16
AF = mybir.ActivationFunctionType
ALU = mybir.AluOpType
AX = mybir.AxisListType

D = 1024
DFF = 2816
NFC = DFF // 128
TT = 512


EPS = 1e-6
GELU_C = 0.7978845608028654


class _Op:
    __slots__ = ("eng", "fn", "dma", "deps", "ms", "dsem", "dval", "idx", "has_dep", "sw", "gen", "swk")

    def __init__(self, eng, fn, dma):
        self.eng = eng
        self.fn = fn
        self.dma = dma
        self.deps = ()
        self.ms = 0
        self.dsem = None
        self.dval = 0
        self.has_dep = False
        self.sw = None
        self.gen = 0
        self.swk = 0


class Prog:
    ENGS = ("pe", "act", "dve", "pool", "sp")

    def __init__(self, nc, n_dma_sems=20, same_engine_sync=True):
        self.nc = nc
        self.ops = {e: [] for e in self.ENGS}
        self.last_w = {}
        self.readers = {}
        self.n_dma_sems = n_dma_sems
        self.same_engine_sync = same_engine_sync
        self.dma_rr = 0
        self.dma_last = [None] * n_dma_sems
        self.dma_cnt = [0] * n_dma_sems
        self.sw_gen = {}
        self.sw_count = 0

    def add(self, eng, fn, reads=(), writes=(), dma=False, sw=None):
        op = _Op(eng, fn, dma)
        op.idx = len(self.ops[eng])
        self.ops[eng].append(op)
        deps = set()
        for k in reads:
            w = self.last_w.get(k)
            if w is not None:
                deps.add(w)
        for k in writes:
            w = self.last_w.get(k)
            if w is not None:
                deps.add(w)
            rd = self.readers.get(k)
            if rd:
                deps.update(rd.values())
        for k in reads:
            self.readers.setdefault(k, {})[eng] = op
        for k in writes:
            self.last_w[k] = op
            self.readers[k] = {}
        if sw is not None:
            op.sw = sw
            op.gen = self.sw_gen.get(sw, 0)
            self.sw_gen[sw] = op.gen + 1
            self.sw_count += 1
            op.swk = self.sw_count
        elif dma:
            s = self.dma_rr
            self.dma_rr = (self.dma_rr + 1) % self.n_dma_sems
            prev = self.dma_last[s]
            if prev is not None:
                deps.add(prev)
            self.dma_cnt[s] += 1
            op.dsem = s
            op.dval = 16 * self.dma_cnt[s]
            self.dma_last[s] = op
        deps.discard(op)
        op.deps = deps
        for d in deps:
            d.has_dep = True
        return op

    def pe(self, fn, r=(), w=()):
        return self.add("pe", fn, r, w)

    def act(self, fn, r=(), w=()):
        return self.add("act", fn, r, w)

    def dve(self, fn, r=(), w=()):
        return self.add("dve", fn, r, w)

    def pool(self, fn, r=(), w=()):
        return self.add("pool", fn, r, w)

    def dma(self, q, fn, r=(), w=()):
        return self.add(q, fn, r, w, dma=True)

    def dma_sw(self, slot, fn, r=(), w=()):
        return self.add("pool", fn, r, w, dma=True, sw=slot)

    def emit(self):
        nc = self.nc
        for e in self.ENGS:
            c = 0
            for op in self.ops[e]:
                if op.has_dep and not op.dma:
                    c += 1
                    op.ms = c
        with contextlib.ExitStack() as st:
            esem = {e: st.enter_context(nc.semaphore("ms_" + e)) for e in self.ENGS}
            dsem = [st.enter_context(nc.semaphore("dq%d" % i)) for i in range(self.n_dma_sems)]
            sw_unique = self.sw_count <= 100
            if sw_unique:
                swsem = {k: st.enter_context(nc.semaphore("sw%d" % k)) for k in range(1, self.sw_count + 1)}
            else:
                swsem = {k: st.enter_context(nc.semaphore("sw%s" % k)) for k in sorted(self.sw_gen)}
            block = st.enter_context(nc.Block())
            engobj = {"pe": nc.tensor, "act": nc.scalar, "dve": nc.vector,
                      "pool": nc.gpsimd, "sp": nc.sync}
            self.n_waits = 0

            def body(e):
                eng = engobj[e]
                waited = {}
                for op in self.ops[e]:
                    need = {}
                    for d in op.deps:
                        if d.sw is not None:
                            if sw_unique:
                                if not waited.get(("swu", d.swk)):
                                    waited[("swu", d.swk)] = 1
                                    eng.wait_ge(swsem[d.swk], 16)
                                    self.n_waits += 1
                            elif waited.get(("sw", d.sw), 0) < d.gen + 1:
                                waited[("sw", d.sw)] = d.gen + 1
                                eng.wait_ge(swsem[d.sw], 16 * (d.gen + 1))
                                self.n_waits += 1
                            continue
                        if d.dma:
                            key = ("d", d.dsem)
                            val = d.dval
                        else:
                            if d.eng == e and (e == "pe" or not self.same_engine_sync):
                                continue
                            key = ("e", d.eng)
                            val = d.ms
                        if need.get(key, 0) < val:
                            need[key] = val
                    for key, val in need.items():
                        if waited.get(key, 0) >= val:
                            continue
                        waited[key] = val
                        sem = dsem[key[1]] if key[0] == "d" else esem[key[1]]
                        eng.wait_ge(sem, val)
                        self.n_waits += 1
                    if op.sw is not None:
                        op.fn().then_inc(swsem[op.swk if sw_unique else op.sw], 16)
                        continue
                    inst = op.fn()
                    if op.dma:
                        inst.then_inc(dsem[op.dsem], 16)
                    elif op.ms:
                        inst.then_inc(esem[e], 1)

            @block.tensor
            def _(t):
                body("pe")

            @block.scalar
            def _(t):
                body("act")

            @block.vector
            def _(t):
                body("dve")

            @block.gpsimd
            def _(t):
                body("pool")

            @block.sync
            def _(t):
                body("sp")


class WStream:
    def __init__(self, tag, ch, nslot):
        self.tag = tag
        self.CH = ch
        self.NSLOT = nslot
        self.phases = {}
        self.total_chunks = 0
        self.dry = True
        self.cur_key = None
        self.pos = 0

    def begin_phase(self, key):
        self.cur_key = key
        self.pos = 0
        if self.dry:
            self.recording = key not in self.phases
            if self.recording:
                self.phases[key] = {"blocks": [], "nch": 0, "base": self.total_chunks}
        else:
            ph = self.phases[key]
            assert self.order[self.oi] == key, (self.order[self.oi], key)
            self.gbase = self.obase[self.oi]
            self.oi += 1

    def end_phase(self):
        if self.dry and self.recording:
            ph = self.phases[self.cur_key]
            ph["nch"] = (self.pos + self.CH - 1) // self.CH
            self.total_chunks += ph["nch"]

    def arm(self, p, nc, wst, ring, order):
        self.dry = False
        self.p, self.nc, self.wst, self.ring = p, nc, wst, ring
        self.order = order
        self.oi = 0
        self.G = []
        self.obase = []
        for key in order:
            ph = self.phases[key]
            self.obase.append(len(self.G))
            self.G.extend(range(ph["base"], ph["base"] + ph["nch"]))
        self.cur = -1
        self.issued = -1

    def _issue(self, g):
        if g >= len(self.G) or g <= self.issued:
            return
        assert g == self.issued + 1
        self.issued = g
        nc, ring, wst = self.nc, self.ring, self.wst
        slot = g % self.NSLOT
        c = self.G[g]
        a = self.CH // 1024
        self.p.dma_sw(self.tag + str(slot),
                      lambda: nc.gpsimd.dma_start(
                          out=ring[slot][:, :].rearrange("p (a b) -> p a b", a=a),
                          in_=wst[c].rearrange("p (a b) -> p a b", a=a)),
                      w=[("ring" + self.tag, slot)])

    def get(self, name, l, r0, c0, ncols=128, reserve=0):
        CH = self.CH
        if (self.pos % CH) + max(ncols, reserve) > CH:
            self.pos = (self.pos // CH + 1) * CH
        if self.dry:
            if self.recording:
                self.phases[self.cur_key]["blocks"].append((name, l, r0, c0, ncols, self.pos))
            self.pos += ncols
            return None, ("ring" + self.tag, 0)
        chunk = self.pos // CH
        off = self.pos % CH
        self.pos += ncols
        g = self.gbase + chunk
        if g != self.cur:
            if self.cur < 0:
                for i in range(self.NSLOT):
                    self._issue(i)
            else:
                assert g == self.cur + 1, (g, self.cur)
                self._issue(g - 1 + self.NSLOT)
            self.cur = g
        slot = g % self.NSLOT
        return self.ring[slot][:, off:off + ncols], ("ring" + self.tag, slot)


def _small_layout(L, final):
    off = {}
    n = 0

    def put(name, w):
        nonlocal n
        off[name] = n
        n += w
    for l in range(L):
        put(("g1", l), 8)
        put(("gm", l), 8)
        put(("g2", l), 8)
        put(("pscale", l), 2)
        put(("convw", l), 8)
        put(("convb", l), 2)
        put(("ba", l), 2)
        put(("bx", l), 2)
        put(("lam", l), 2)
    if final:
        put(("gf", 0), 8)
    return off, n


def _vec2(v):
    return np.ascontiguousarray(v.reshape(2, 128).T)


def _vec8(v):
    return np.ascontiguousarray(v.reshape(8, 128).T)


def _blockdiag(w2):
    o = np.zeros((128, 128), np.float32)
    o[0:64, 0:64] = w2[0]
    o[64:128, 64:128] = w2[1]
    return o


def _consts():
    j = np.arange(128)[:, None]
    s = np.arange(128)[None, :]
    ones = np.ones((128, 128), np.float32)
    onesneg = -ones
    trineg = -(j >= s).astype(np.float32)
    ident = np.eye(128, dtype=np.float32)
    negmask = np.where(j >= s, -30000.0, 0.0).astype(np.float32)
    return np.concatenate([ones, onesneg, trineg, ident, negmask], axis=1)


CHF, NSF = 3072, 3
CHM, NSM = 2048, 3
FGROUPS = ((0, 8), (8, 15), (15, 22))


def _merge(gens):
    vt = [0.0] * len(gens)
    alive = [True] * len(gens)
    while any(alive):
        i = min((j for j in range(len(gens)) if alive[j]), key=lambda j: vt[j])
        try:
            c = next(gens[i])
            vt[i] += (c or 0.1)
        except StopIteration:
            alive[i] = False


def _chain(*gens):
    for g in gens:
        yield from g


class _Z:
    def __getitem__(self, k):
        return self


def _schedule(NT, L):
    if NT < 2 or NT % 2:
        steps = []
        for t in range(NT):
            steps.append([[("F", t, 0, 1)]])
            for l in range(L):
                steps.append([[("M", t, l)]])
                nxt = [("F", t, l, 2)]
                nxt.append(("F", t, l + 1, 1) if l + 1 < L else ("FIN", t))
                steps.append([nxt])
            if t + 2 < NT:
                steps.append([[("LD", t + 2)]])
        return steps
    steps = [[[("F", 0, 0, 1)]]]
    carry = []
    for a in range(0, NT, 2):
        b, c, d = a + 1, a + 2, a + 3
        for l in range(L):
            if l == 0:
                fa = carry + [("F", b, 0, 1)]
            else:
                fa = [("F", b, l - 1, 2), ("F", b, l, 1)]
            steps.append([[("M", a, l)], fa])
            fb = [("F", a, l, 2)]
            if l + 1 < L:
                fb.append(("F", a, l + 1, 1))
            else:
                fb.append(("FIN", a))
                if c < NT:
                    fb += [("LD", c), ("F", c, 0, 1)]
            steps.append([[("M", b, l)], fb])
        carry = [("F", b, L - 1, 2), ("FIN", b)]
        if d < NT:
            carry.append(("LD", d))
    steps.append([carry])
    return steps


def build(S, L, final):
    NT = S // TT
    NKB = S // 128
    soff, NS = _small_layout(L, final)
    steps = _schedule(NT, L)

    wsF = WStream("F", CHF, NSF)
    wsM = WStream("M", CHM, NSM)
    G = _Gen(None, Prog(None), wsF, wsM, L, final, S, None, soff)
    for l in range(L):
        for which in (1, 2):
            for _ in G.g_ffn(0, l, which):
                pass
        for _ in G.g_mixer(0, l):
            pass
    orderF, orderM = [], []
    for st in steps:
        for stream in st:
            for ph in stream:
                if ph[0] == "F":
                    orderF.append(("F", ph[2], ph[3]))
                elif ph[0] == "M":
                    orderM.append(("M", ph[2]))

    nc = bass.Bass("TRN2", target_bir_lowering=False)
    x_in = nc.dram_tensor("xT", [D, S], F32, kind="ExternalInput").ap()
    wstF = nc.dram_tensor("wstF", [wsF.total_chunks, 128, CHF], F32, kind="ExternalInput").ap()
    wstM = nc.dram_tensor("wstM", [wsM.total_chunks, 128, CHM], F32, kind="ExternalInput").ap()
    small_in = nc.dram_tensor("small", [128, NS], F32, kind="ExternalInput").ap()
    cst_in = nc.dram_tensor("cst", [128, 5 * 128], F32, kind="ExternalInput").ap()
    bd_in = nc.dram_tensor("bd", [L, 6, 128, 128], F32, kind="ExternalInput").ap()
    ws_in = nc.dram_tensor("wsT", [L, 4, 128, 128], F32, kind="ExternalInput").ap()
    bs_in = nc.dram_tensor("bsb", [L, 2, 128, 128], F32, kind="ExternalInput").ap()
    y_out = nc.dram_tensor("yT", [D, S], F32, kind="ExternalOutput").ap()

    with contextlib.ExitStack() as st:
        def sb(name, shape, dt):
            return st.enter_context(nc.sbuf_tensor(name, shape, dt))

        T = {}
        T["xT"] = [sb("xT_sb%d" % i, [128, 8, TT], F32) for i in range(2)]
        T["xnF"] = sb("xnF", [128, 8, TT], BF16)
        T["aT"] = sb("aT", [128, 8, TT], BF16)
        T["Fs"] = [sb("Fs%d" % i, [128, TT], F32) for i in range(2)]
        T["xnM"] = sb("xnM", [128, 8, TT], BF16)
        T["yT"] = sb("yTm", [128, 8, TT], BF16)
        T["kc"] = sb("kc", [128, L * 2, S], BF16)
        T["vc"] = sb("vc", [128, L * NKB, 256], BF16)
        T["ringF"] = [sb("ringF%d" % i, [128, CHF], BF16) for i in range(NSF)]
        T["ringM"] = [sb("ringM%d" % i, [128, CHM], BF16) for i in range(NSM)]
        T["small"] = sb("small_sb", [128, NS], F32)
        T["cst"] = sb("cstb", [128, 5 * 128], BF16)
        T["bd"] = sb("bdb", [128, L * 6, 128], BF16)
        T["wsT"] = sb("wsTb", [128, L * 4, 128], BF16)
        T["bsb"] = sb("bsb_sb", [128, L * 2, 128], F32)
        T["cneg"] = sb("cneg", [128, L * 4], F32)
        T["nb"] = sb("nb", [128, L * 4], F32)
        T["epsb"] = sb("epsb", [128, 1], F32)
        T["icnt"] = sb("icnt", [128, 2, 16], F32)
        T["poolh"] = sb("poolh", [128, L * 2, 16], F32)
        T["lruh"] = sb("lruh", [128, L * 2, 3], F32)
        T["hst"] = sb("hst", [128, L * 2], F32)
        T["F"] = [sb("F%d" % i, [128, TT + 16], F32) for i in range(7)]
        T["B"] = [sb("B%d" % i, [128, TT], BF16) for i in range(6)]
        T["S"] = [sb("S%d" % i, [128, TT], BF16) for i in range(2)]
        T["S32"] = [sb("S32_%d" % i, [128, TT], F32) for i in range(2)]
        T["E"] = [sb("E%d" % i, [128, TT], F32) for i in range(2)]
        T["qz"] = [sb("qz%d" % i, [128, TT], BF16) for i in range(4)]
        T["st"] = sb("stat", [128, 32], F32)
        T["PS"] = [st.enter_context(nc.psum_tensor("ps%d" % i, [128, 512], F32)) for i in range(8)]
        T["x_in"] = x_in
        T["y_out"] = y_out

        p = Prog(nc)
        wsF.arm(p, nc, wstF, T["ringF"], orderF)
        wsM.arm(p, nc, wstM, T["ringM"], orderM)
        F = T["F"]

        p.dma("sp", lambda: nc.sync.dma_start(out=T["small"][:, :], in_=small_in[:, :]), w=["small"])
        for i in range(5):
            stg = F[i % 2]
            p.dma("sp", lambda i=i, stg=stg: nc.sync.dma_start(out=stg[:, 0:128], in_=cst_in[:, i * 128:(i + 1) * 128]),
                  w=[("F", i % 2)])
            p.dve(lambda i=i, stg=stg: nc.vector.tensor_copy(out=T["cst"][:, i * 128:(i + 1) * 128], in_=stg[:, 0:128]),
                  r=[("F", i % 2)], w=["cst"])
        k_ = 0
        for l in range(L):
            for i in range(6):
                stg = F[k_ % 2]
                p.dma("sp", lambda l=l, i=i, stg=stg: nc.sync.dma_start(out=stg[:, 0:128], in_=bd_in[l, i]),
                      w=[("F", k_ % 2)])
                p.dve(lambda l=l, i=i, stg=stg: nc.vector.tensor_copy(out=T["bd"][:, l * 6 + i, :], in_=stg[:, 0:128]),
                      r=[("F", k_ % 2)], w=["bd"])
                k_ += 1
            for h in range(4):
                stg = F[k_ % 2]
                p.dma("sp", lambda l=l, h=h, stg=stg: nc.sync.dma_start(out=stg[:, 0:128], in_=ws_in[l, h]),
                      w=[("F", k_ % 2)])
                p.pool(lambda l=l, h=h, stg=stg: nc.gpsimd.affine_select(
                    out=T["wsT"][:, l * 4 + h, :], in_=stg[:, 0:128], pattern=[[1, 128]],
                    compare_op=ALU.is_ge, fill=0.0, base=0, channel_multiplier=-1),
                    r=[("F", k_ % 2)], w=["wsT"])
                k_ += 1
            for c in range(2):
                p.dma("sp", lambda l=l, c=c: nc.sync.dma_start(out=T["bsb"][:, l * 2 + c, :], in_=bs_in[l, c]),
                      w=["bsb"])
            lo = soff[("lam", l)]
            p.act(lambda l=l, lo=lo: nc.scalar.activation(out=T["st"][:, 0:2], in_=T["small"][:, lo:lo + 2],
                                                           func=AF.Exp, scale=-1.0),
                  r=["small"], w=["st"])
            p.act(lambda l=l: nc.scalar.activation(out=T["st"][:, 2:4], in_=T["st"][:, 0:2], func=AF.Ln,
                                                    bias=1.0, scale=1.0),
                  r=["st"], w=["st"])
            p.dve(lambda l=l: nc.vector.tensor_scalar(out=T["cneg"][:, 2 * l:2 * l + 2], in0=T["st"][:, 2:4],
                                                      scalar1=-8.0, scalar2=None, op0=ALU.mult),
                  r=["st"], w=["cneg"])
            p.dve(lambda l=l: nc.vector.tensor_scalar(out=T["cneg"][:, 2 * L + 2 * l:2 * L + 2 * l + 2],
                                                      in0=T["st"][:, 2:4], scalar1=8.0, scalar2=None, op0=ALU.mult),
                  r=["st"], w=["cneg"])
        p.dve(lambda: nc.vector.memset(T["epsb"][:, :], EPS), w=["epsb"])
        for l in range(L):
            for k2_, nm in enumerate(("ba", "bx")):
                so = soff[(nm, l)]
                p.dve(lambda l=l, k2_=k2_, so=so: nc.vector.tensor_scalar(
                    out=T["nb"][:, l * 4 + 2 * k2_:l * 4 + 2 * k2_ + 2], in0=T["small"][:, so:so + 2],
                    scalar1=-1.0, scalar2=None, op0=ALU.mult), r=["small"], w=["nb"])
        wins = (2, 4, 8, 16)
        for c in range(2):
            for hf in range(2):
                g = 2 * c + hf
                p.dve(lambda c=c, hf=hf, g=g: nc.vector.memset(T["icnt"][64 * hf:64 * hf + 64, c, :], 1.0 / wins[g]),
                      w=["icnt"])
        for t in range(15):
            for c in range(2):
                for hf in range(2):
                    g = 2 * c + hf
                    if wins[g] > t + 1:
                        p.dve(lambda c=c, hf=hf, t=t: nc.vector.memset(
                            T["icnt"][64 * hf:64 * hf + 64, c, t:t + 1], 1.0 / (t + 1)), w=["icnt"])
        p.dve(lambda: nc.vector.memset(T["poolh"][:, :, :], 0.0), w=["poolh"])
        p.dve(lambda: nc.vector.memset(T["lruh"][:, :, :], 0.0), w=["lruh"])
        p.dve(lambda: nc.vector.memset(T["hst"][:, :], 0.0), w=["hst"])
        for h in range(4):
            p.dve(lambda h=h: nc.vector.memset(T["qz"][h][:, :], 0.0), w=[("qz", h)])

        G = _Gen(nc, p, wsF, wsM, L, final, S, T, soff)
        G.load_x(0)
        if NT > 1:
            G.load_x(1)

        def _ld(t_):
            G.load_x(t_)
            yield 0.1
        for stp in steps:
            gens = []
            for stream in stp:
                parts = []
                for ph in stream:
                    if ph[0] == "F":
                        parts.append(G.g_ffn(ph[1], ph[2], ph[3]))
                    elif ph[0] == "M":
                        parts.append(G.g_mixer(ph[1], ph[2]))
                    elif ph[0] == "LD":
                        parts.append(_ld(ph[1]))
                    else:
                        parts.append(G.g_final(ph[1]))
                gens.append(_chain(*parts))
            _merge(gens)
        p.add("sp", lambda: nc.sync.nop(), reads=["y_hbm"])
        p.emit()
    packs = {"F": wsF, "M": wsM}
    return nc, packs, NS, soff


class _Gen:
    def __init__(self, nc, p, wsF, wsM, L, final, S, T, soff):
        self.nc, self.p, self.wsF, self.wsM = nc, p, wsF, wsM
        self.L, self.final, self.S, self.soff = L, final, S, soff
        if T is None:
            T = {k: _Z() for k in ("xT", "xnF", "xnM", "aT", "yT", "kc", "vc", "small", "cst", "bd", "wsT", "bsb",
                                   "cneg", "nb", "icnt", "poolh", "lruh", "hst", "st", "x_in", "y_out",
                                   "epsb")}
            T.update({"F": [None] * 7, "B": [None] * 6, "S": [None] * 2, "S32": [None] * 2, "E": [None] * 2,
                      "PS": [None] * 8, "Fs": [None] * 2, "qz": [None] * 4})
        self.T = T
        self.fbank = 0
        self.zrr = 0

    def load_x(self, t_):
        nc, p, T = self.nc, self.p, self.T
        buf = T["xT"][t_ % 2]
        p.dma("sp", lambda: nc.sync.dma_start(
            out=buf[:, :, :], in_=T["x_in"].rearrange("(c q) t -> q c t", q=128)[:, :, t_ * TT:t_ * TT + TT]),
            w=[("xT", t_ % 2, c) for c in range(8)])

    def _fb(self):
        v = (5, 6, 7)[self.fbank % 3]
        self.fbank += 1
        return v

    def rmsnorm(self, tt, gname, l, xn, xnk, bank, rbuf, rkey, out_fp32=False):
        nc, p, T = self.nc, self.p, self.T
        PS, small, cst = T["PS"], T["small"], T["cst"]
        xT = T["xT"][tt % 2]
        XP = tt % 2
        go = self.soff[(gname, l)]
        for c in range(8):
            if c % 2 == 0:
                p.act(lambda c=c: nc.scalar.activation(out=xn[:, c, :], in_=xT[:, c, :], func=AF.Square),
                      r=[("xT", XP, c)], w=[(xnk, c)])
            else:
                p.pool(lambda c=c: nc.gpsimd.tensor_tensor(out=xn[:, c, :], in0=xT[:, c, :], in1=xT[:, c, :],
                                                           op=ALU.mult),
                       r=[("xT", XP, c)], w=[(xnk, c)])
            p.pe(lambda c=c: nc.tensor.matmul(PS[bank][:, :], lhsT=cst[:, 0:128], rhs=xn[:, c, :],
                                              start=(c == 0), stop=(c == 7)),
                 r=[(xnk, c), "cst"], w=[("ps", bank)])
            yield 0.5
        p.act(lambda: nc.scalar.activation(out=rbuf[:, 0:TT], in_=PS[bank][:, :], func=AF.Ln,
                                           bias=T["epsb"][:, 0:1], scale=1.0 / D),
              r=[("ps", bank), "epsb"], w=[rkey])
        p.act(lambda: nc.scalar.activation(out=rbuf[:, 0:TT], in_=rbuf[:, 0:TT], func=AF.Exp, scale=-0.5),
              r=[rkey], w=[rkey])
        yield 1.0
        for c in range(8):
            dst = xT if out_fp32 else xn
            dk = ("xT", XP, c) if out_fp32 else (xnk, c)
            p.dve(lambda c=c, dst=dst: nc.vector.scalar_tensor_tensor(
                out=dst[:, c, :], in0=xT[:, c, :], scalar=small[:, go + c:go + c + 1], op0=ALU.mult,
                in1=rbuf[:, 0:TT], op1=ALU.mult),
                r=[("xT", XP, c), rkey, "small"], w=[dk])
            yield 0.5

    def g_ffn(self, tt, l, which):
        nc, p, T, ws = self.nc, self.p, self.T, self.wsF
        PS, xn, aT, Fs = T["PS"], T["xnF"], T["aT"], T["Fs"]
        xT = T["xT"][tt % 2]
        XP = tt % 2
        wi, wo = {1: ("ffn1_w_in", "ffn1_w_out"), 2: ("ffn2_w_in", "ffn2_w_out")}[which]
        ws.begin_phase(("F", l, which))
        yield from self.rmsnorm(tt, "g%d" % which, l, xn, "xnF", self._fb(), Fs[0], ("Fs", 0))
        for (f0, f1) in FGROUPS:
            for fc in range(f0, f1):
                j = fc - f0
                bg, bu = self._fb(), self._fb()
                for half, bank in ((0, bg), (1, bu)):
                    for kc in range(8):
                        wap, wk = ws.get(wi, l, kc * 128, half * DFF + fc * 128)
                        p.pe(lambda wap=wap, kc=kc, bank=bank: nc.tensor.matmul(
                            PS[bank][:, :], lhsT=wap, rhs=xn[:, kc, :], start=(kc == 0), stop=(kc == 7)),
                            r=[wk, ("xnF", kc)], w=[("ps", bank)])
                    yield 2.1
                f = Fs[j % 2]
                fk = ("Fs", j % 2)
                p.act(lambda bg=bg, f=f: nc.scalar.activation(out=f[:, :], in_=PS[bg][:, :], func=AF.Exp, scale=-1.0),
                      r=[("ps", bg)], w=[fk])
                p.act(lambda f=f: nc.scalar.activation(out=f[:, :], in_=f[:, :], func=AF.Ln, bias=1.0, scale=1.0),
                      r=[fk], w=[fk])
                p.act(lambda f=f: nc.scalar.activation(out=f[:, :], in_=f[:, :], func=AF.Exp, scale=-1.0),
                      r=[fk], w=[fk])
                p.dve(lambda bg=bg, f=f: nc.vector.tensor_tensor(out=f[:, :], in0=PS[bg][:, :], in1=f[:, :],
                                                                 op=ALU.mult),
                      r=[("ps", bg), fk], w=[fk])
                p.dve(lambda bu=bu, f=f, j=j: nc.vector.tensor_tensor(out=aT[:, j, :], in0=PS[bu][:, :],
                                                                       in1=f[:, :], op=ALU.mult),
                      r=[("ps", bu), fk], w=[("aT", j)])
            ng = f1 - f0
            for dc in range(8):
                bank = self._fb()
                for j in range(ng):
                    fc = f0 + j
                    wap, wk = ws.get(wo, l, fc * 128, dc * 128)
                    p.pe(lambda wap=wap, j=j, bank=bank, ng=ng: nc.tensor.matmul(
                        PS[bank][:, :], lhsT=wap, rhs=aT[:, j, :], start=(j == 0), stop=(j == ng - 1)),
                        r=[wk, ("aT", j)], w=[("ps", bank)])
                p.dve(lambda dc=dc, bank=bank: nc.vector.scalar_tensor_tensor(
                    out=xT[:, dc, :], in0=PS[bank][:, :], scalar=0.5, op0=ALU.mult, in1=xT[:, dc, :], op1=ALU.add),
                    r=[("ps", bank), ("xT", XP, dc)], w=[("xT", XP, dc)])
                yield 0.26 * ng
        ws.end_phase()

    def g_final(self, tt):
        nc, p, T = self.nc, self.p, self.T
        xT = T["xT"][tt % 2]
        XP = tt % 2
        t0 = tt * TT
        if self.final:
            yield from self.rmsnorm(tt, "gf", 0, T["xnF"], "xnF", self._fb(), T["Fs"][0], ("Fs", 0), out_fp32=True)
        p.dma("sp", lambda: nc.sync.dma_start(
            out=T["y_out"].rearrange("(c q) t -> q c t", q=128)[:, :, t0:t0 + TT], in_=xT[:, :, :]),
            r=[("xT", XP, c) for c in range(8)], w=["y_hbm"])
        yield 0.1

    def proj_fm(self, l, col0, bank=3):
        nc, p, T, ws = self.nc, self.p, self.T, self.wsM
        PS, xn = T["PS"], T["xnM"]
        for kc in range(8):
            wap, wk = ws.get("mix_w_in", l, kc * 128, col0)
            p.pe(lambda wap=wap, kc=kc: nc.tensor.matmul(
                PS[bank][:, :], lhsT=wap, rhs=xn[:, kc, :], start=(kc == 0), stop=(kc == 7)),
                r=[wk, ("xnM", kc)], w=[("ps", bank)])
        return bank

    def proj_tm_blocks(self, l, col0):
        return [self.wsM.get("mix_w_in", l, kc * 128, col0, 256, reserve=(8 - kc) * 256) for kc in range(8)]

    def proj_tm_tb(self, waps, tb, bank, half):
        nc, p, T = self.nc, self.p, self.T
        PS, xn = T["PS"], T["xnM"]
        for kc in range(8):
            wap, wk = waps[kc]
            p.pe(lambda wap=wap, kc=kc: nc.tensor.matmul(
                PS[bank][:, half * 256:half * 256 + 256], lhsT=xn[:, kc, tb * 128:(tb + 1) * 128], rhs=wap,
                start=(kc == 0), stop=(kc == 7), skip_group_check=True),
                r=[wk, ("xnM", kc)], w=[("ps", bank)])

    def gelu(self, dst_fn, dkeys, src_fn, skeys, ia, ib, width):
        nc, p, T = self.nc, self.p, self.T
        F = T["F"]
        fa, fb = F[ia], F[ib]
        ka, kb = ("F", ia), ("F", ib)
        w = width
        cst_ = w / 1000.0
        p.dve(lambda: nc.vector.tensor_copy(out=fa[:, 0:w], in_=src_fn()), r=skeys, w=[ka])
        yield cst_
        p.pool(lambda: nc.gpsimd.tensor_tensor(out=fb[:, 0:w], in0=fa[:, 0:w], in1=fa[:, 0:w], op=ALU.mult),
               r=[ka], w=[kb])
        yield cst_
        p.dve(lambda: nc.vector.tensor_scalar(out=fb[:, 0:w], in0=fb[:, 0:w], scalar1=0.044715, scalar2=1.0,
                                              op0=ALU.mult, op1=ALU.add), r=[kb], w=[kb])
        yield cst_
        p.pool(lambda: nc.gpsimd.tensor_tensor(out=fb[:, 0:w], in0=fb[:, 0:w], in1=fa[:, 0:w], op=ALU.mult),
               r=[ka, kb], w=[kb])
        yield cst_
        p.act(lambda: nc.scalar.activation(out=fb[:, 0:w], in_=fb[:, 0:w], func=AF.Exp, scale=-2.0 * GELU_C),
              r=[kb], w=[kb])
        yield cst_
        p.pool(lambda: nc.gpsimd.tensor_scalar(out=fb[:, 0:w], in0=fb[:, 0:w], scalar1=1.0, scalar2=1.0,
                                               op0=ALU.mult, op1=ALU.add), r=[kb], w=[kb])
        yield cst_
        p.dve(lambda: nc.vector.reciprocal(out=fb[:, 0:w], in_=fb[:, 0:w]), r=[kb], w=[kb])
        yield 3 * cst_
        p.pool(lambda: nc.gpsimd.tensor_tensor(out=dst_fn(), in0=fa[:, 0:w], in1=fb[:, 0:w], op=ALU.mult),
               r=[ka, kb], w=dkeys)
        yield cst_

    def g_mixer(self, tt, l):
        nc, p, T, ws = self.nc, self.p, self.T, self.wsM
        L, S, soff = self.L, self.S, self.soff
        NKB = S // 128
        t0 = tt * TT
        PS, xn, yT = T["PS"], T["xnM"], T["yT"]
        xT = T["xT"][tt % 2]
        XP = tt % 2
        small, cst = T["small"], T["cst"]
        F, B, E = T["F"], T["B"], T["E"]
        ONESNEG = lambda: cst[:, 128:256]
        TRINEG = lambda: cst[:, 256:384]
        IDENT = lambda: cst[:, 384:512]
        NEGMASK = lambda: cst[:, 512:640]
        C0 = {"pool": 0, "q": 256, "k": 512, "v": 768, "su": 1024, "sv": 1280, "lx": 1536, "lg": 1792}
        kc_t, vc_t, qz = T["kc"], T["vc"], T["qz"]
        wins = (2, 4, 8, 16)
        OB, NB_, ACC = 3, 4, 2
        gen = self

        ws.begin_phase(("M", l))
        yield from self.rmsnorm(tt, "gm", l, xn, "xnM", NB_, F[6], ("F", 6))

        for c in range(2):
            bank = self.proj_fm(l, C0["k"] + c * 128)
            p.dve(lambda bank=bank, c=c: nc.vector.tensor_copy(out=kc_t[:, l * 2 + c, t0:t0 + TT], in_=PS[bank][:, :]),
                  r=[("ps", bank)], w=[("kc", l, c)])
            yield 2.1
            bank = self.proj_fm(l, C0["q"] + c * 128)
            for hf in range(2):
                h_ = 2 * c + hf
                sl = slice(64 * hf, 64 * hf + 64)
                p.act(lambda bank=bank, h_=h_, sl=sl: nc.scalar.activation(
                    out=qz[h_][sl, :], in_=PS[bank][sl, :], func=AF.Copy, scale=0.125),
                    r=[("ps", bank)], w=[("qz", h_)])
            yield 2.1
        waps = self.proj_tm_blocks(l, C0["v"])
        for tb in range(4):
            half = tb % 2
            self.proj_tm_tb(waps, tb, OB, half)
            kb = tt * 4 + tb
            p.dve(lambda half=half, kb=kb: nc.vector.tensor_copy(
                out=vc_t[:, l * NKB + kb, :], in_=PS[OB][:, half * 256:half * 256 + 256]),
                r=[("ps", OB)], w=[("vc", l)])
            yield 1.1

        nkb = 4 * tt + 4

        def g_attn():
            for hp in range(2):
                hs = (2 * hp, 2 * hp + 1)
                ulist = [(kb, h) for kb in range(nkb - 1, -1, -1) for h in hs]
                state = {"ei": 0}

                def stage1(u):
                    kb, h = u
                    c, pb = h // 2, 64 * (h % 2)
                    j = kb - 4 * tt
                    c0 = max(0, 128 * j)
                    zb = gen.zrr % 2
                    gen.zrr += 1
                    ei = state["ei"]
                    state["ei"] = ei + 1
                    e2 = ei % 2
                    ef, spb, ab = E[e2], B[2 + e2], B[4 + e2]
                    state[u] = (zb, ef, spb, ab, e2, c0)
                    p.pe(lambda: nc.tensor.matmul(
                        PS[zb][:, c0:TT], lhsT=kc_t[:, l * 2 + c, kb * 128:(kb + 1) * 128],
                        rhs=qz[h][:, c0:TT], start=True, stop=False, skip_group_check=True),
                        r=[("kc", l, c), ("qz", h)], w=[("ps", zb)])
                    if j >= 0:
                        p.pe(lambda: nc.tensor.matmul(
                            PS[zb][:, c0:c0 + 128], lhsT=IDENT(), rhs=NEGMASK(), start=False, stop=False,
                            skip_group_check=True),
                            r=["cst"], w=[("ps", zb)])
                    p.act(lambda: nc.scalar.activation(out=ef[:, c0:TT], in_=PS[zb][:, c0:TT], func=AF.Exp),
                          r=[("ps", zb)], w=[("E", e2)])
                    p.act(lambda: nc.scalar.activation(out=spb[:, c0:TT], in_=ef[:, c0:TT], func=AF.Ln,
                                                       bias=1.0, scale=1.0),
                          r=[("E", e2)], w=[("B", 2 + e2)])

                def stage2(u):
                    kb, h = u
                    zb, ef, spb, ab, e2, c0 = state.pop(u)
                    j = kb - 4 * tt
                    first = (kb == nkb - 1)
                    hh = h % 2
                    Sh = T["S"][hh]
                    S32 = T["S32"][hh]
                    cs = c0 + 128 if j >= 0 else 0
                    p.pe(lambda: nc.tensor.matmul(
                        PS[zb][:, c0:TT], lhsT=TRINEG(), rhs=spb[:, c0:TT], start=False, stop=first,
                        skip_group_check=True),
                        r=["cst", ("B", 2 + e2)], w=[("ps", zb)])
                    if not first:
                        p.pe(lambda: nc.tensor.matmul(
                            PS[zb][:, cs:TT], lhsT=ONESNEG(), rhs=Sh[:, cs:TT], start=False, stop=True,
                            skip_group_check=True),
                            r=["cst", ("S", hh)], w=[("ps", zb)])
                    if kb > 0:
                        if first:
                            p.pool(lambda: nc.gpsimd.tensor_copy(out=S32[:, c0:TT], in_=spb[:, c0:TT]),
                                   r=[("B", 2 + e2)], w=[("S32", hh)])
                        else:
                            p.pool(lambda: nc.gpsimd.tensor_tensor(out=S32[:, cs:TT], in0=S32[:, cs:TT],
                                                                   in1=spb[:, cs:TT], op=ALU.add),
                                   r=[("B", 2 + e2), ("S32", hh)], w=[("S32", hh)])
                            if j >= 0:
                                p.pool(lambda: nc.gpsimd.tensor_copy(out=S32[:, c0:cs], in_=spb[:, c0:cs]),
                                       r=[("B", 2 + e2)], w=[("S32", hh)])
                        p.dve(lambda: nc.vector.tensor_copy(out=Sh[:, c0:TT], in_=S32[:, c0:TT]),
                              r=[("S32", hh)], w=[("S", hh)])
                    p.act(lambda: nc.scalar.activation(out=ab[:, c0:TT], in_=PS[zb][:, c0:TT], func=AF.Exp),
                          r=[("ps", zb)], w=[("B", 4 + e2)])
                    state[("s3", u)] = (ab, e2, c0)

                def stage3(u):
                    kb, h = u
                    c, pb = h // 2, 64 * (h % 2)
                    ab, e2, c0 = state.pop(("s3", u))
                    first = (kb == nkb - 1)
                    p.pe(lambda: nc.tensor.matmul(
                        PS[ACC][pb:pb + 64, c0:TT], lhsT=vc_t[:, l * NKB + kb, h * 64:(h + 1) * 64],
                        rhs=ab[:, c0:TT], start=first, stop=(kb == 0), skip_group_check=True),
                        r=[("vc", l), ("B", 4 + e2)], w=[("ps", ACC, h % 2)])
                    if kb == 0:
                        p.dve(lambda: nc.vector.tensor_copy(out=yT[pb:pb + 64, 2 + c, :],
                                                            in_=PS[ACC][pb:pb + 64, :]),
                              r=[("ps", ACC, h % 2)], w=[("yT", 2 + c)])

                n = len(ulist)
                for i in range(n + 2):
                    if i < n:
                        stage1(ulist[i])
                    if 1 <= i <= n:
                        stage2(ulist[i - 1])
                    if i >= 2:
                        stage3(ulist[i - 2])
                    yield 0.9

        def g_pool(c):
            X, S2, S4 = F[2], F[3], F[4]
            kX, k2, k4 = ("F", 2), ("F", 3), ("F", 4)
            hidx = l * 2 + c
            W = TT + 16
            p.dve(lambda: nc.vector.tensor_copy(out=X[:, 0:16], in_=T["poolh"][:, hidx, :]),
                  r=[("poolh", hidx)], w=[kX])
            self.proj_fm(l, C0["pool"] + c * 128)
            yield 2.1
            p.dve(lambda: nc.vector.tensor_copy(out=X[:, 16:16 + TT], in_=PS[OB][:, :]),
                  r=[("ps", OB)], w=[kX])
            yield 0.5
            p.dve(lambda: nc.vector.tensor_copy(out=T["poolh"][:, hidx, :], in_=X[:, TT:TT + 16]),
                  r=[kX], w=[("poolh", hidx)])
            p.pool(lambda: nc.gpsimd.tensor_tensor(out=S2[:, 1:W], in0=X[:, 1:W], in1=X[:, 0:W - 1], op=ALU.add),
                   r=[kX], w=[k2])
            yield 0.5
            p.pool(lambda: nc.gpsimd.tensor_tensor(out=S4[:, 3:W], in0=S2[:, 3:W], in1=S2[:, 1:W - 2], op=ALU.add),
                   r=[k2], w=[k4])
            yield 0.5
            if c == 0:
                lvl, lk = (S2, S4), (k2, k4)
            else:
                S8, S16 = F[5], F[3]
                k8, k16 = ("F", 5), ("F", 3)
                p.pool(lambda: nc.gpsimd.tensor_tensor(out=S8[:, 7:W], in0=S4[:, 7:W], in1=S4[:, 3:W - 4],
                                                       op=ALU.add), r=[k4], w=[k8])
                yield 0.5
                p.pool(lambda: nc.gpsimd.tensor_tensor(out=S16[:, 15:W], in0=S8[:, 15:W], in1=S8[:, 7:W - 8],
                                                       op=ALU.add), r=[k8], w=[k16])
                yield 0.5
                lvl, lk = (S8, S16), (k8, k16)
            db = B[0]
            for hf in range(2):
                g = 2 * c + hf
                sl = slice(64 * hf, 64 * hf + 64)
                Lv, kk = lvl[hf], lk[hf]
                p.dve(lambda Lv=Lv, sl=sl, g=g: nc.vector.scalar_tensor_tensor(
                    out=db[sl, :], in0=Lv[sl, 16:16 + TT], scalar=1.0 / wins[g], op0=ALU.mult,
                    in1=X[sl, 16:16 + TT], op1=ALU.subtract),
                    r=[kk, kX], w=[("B", 0)])
                if tt == 0:
                    p.dve(lambda Lv=Lv, sl=sl: nc.vector.tensor_tensor(
                        out=Lv[sl, 16:32], in0=Lv[sl, 16:32], in1=T["icnt"][sl, c, :], op=ALU.mult),
                        r=[kk, "icnt"], w=[kk])
                    p.dve(lambda Lv=Lv, sl=sl: nc.vector.tensor_tensor(
                        out=db[sl, 0:16], in0=Lv[sl, 16:32], in1=X[sl, 16:32], op=ALU.subtract),
                        r=[kk, kX], w=[("B", 0)])
                yield 0.5
            p.pe(lambda: nc.tensor.matmul(PS[OB][:, :], lhsT=T["bd"][:, l * 6 + c, :], rhs=db[:, :],
                                          start=True, stop=True),
                 r=["bd", ("B", 0)], w=[("ps", OB)])
            po = soff[("pscale", l)]
            yield 0.3
            p.dve(lambda: nc.vector.tensor_scalar(
                out=yT[:, c, :], in0=PS[OB][:, :], scalar1=small[:, po + c:po + c + 1], scalar2=None,
                op0=ALU.mult),
                r=[("ps", OB), "small"], w=[("yT", c)])
            yield 0.5

        def g_sgu():
            ug = [F[2], F[3]]
            for c in range(2):
                self.proj_fm(l, C0["su"] + c * 128)
                yield 2.1
                yield from self.gelu(lambda c=c: ug[c][:, 0:TT], [("F", 2 + c)], lambda: PS[OB][:, :],
                                     [("ps", OB)], 4, 5, TT)
            waps = self.proj_tm_blocks(l, C0["sv"])
            stt = T["st"]
            ks = "st"
            vg = F[4]
            kv = ("F", 4)
            vn = B[1]
            for tb in range(4):
                half = tb % 2
                self.proj_tm_tb(waps, tb, OB, half)
                yield 1.1
                yield from self.gelu(lambda: vg[:, 0:256], [kv],
                                     lambda half=half: PS[OB][:, half * 256:half * 256 + 256],
                                     [("ps", OB)], 0, 1, 256)
                p.dve(lambda: nc.vector.tensor_reduce(
                    out=stt[:, 8:12], in_=vg[:, 0:256].rearrange("p (h d) -> p h d", h=4), axis=AX.X, op=ALU.add),
                    r=[kv], w=[ks])
                p.pool(lambda: nc.gpsimd.tensor_tensor(out=vg[:, 256:512], in0=vg[:, 0:256], in1=vg[:, 0:256],
                                                       op=ALU.mult), r=[kv], w=[kv])
                yield 0.5
                p.dve(lambda: nc.vector.tensor_reduce(
                    out=stt[:, 12:16], in_=vg[:, 256:512].rearrange("p (h d) -> p h d", h=4), axis=AX.X,
                    op=ALU.add), r=[kv], w=[ks])
                p.dve(lambda: nc.vector.tensor_scalar(out=stt[:, 8:12], in0=stt[:, 8:12], scalar1=1.0 / 64,
                                                      scalar2=None, op0=ALU.mult), r=[ks], w=[ks])
                p.dve(lambda: nc.vector.tensor_tensor(out=stt[:, 16:20], in0=stt[:, 8:12], in1=stt[:, 8:12],
                                                      op=ALU.mult), r=[ks], w=[ks])
                p.dve(lambda: nc.vector.scalar_tensor_tensor(out=stt[:, 12:16], in0=stt[:, 12:16], scalar=1.0 / 64,
                                                             op0=ALU.mult, in1=stt[:, 16:20], op1=ALU.subtract),
                      r=[ks], w=[ks])
                yield 0.4
                p.act(lambda: nc.scalar.activation(out=stt[:, 12:16], in_=stt[:, 12:16], func=AF.Ln,
                                                   bias=T["epsb"][:, 0:1], scale=1.0), r=[ks, "epsb"], w=[ks])
                p.act(lambda: nc.scalar.activation(out=stt[:, 12:16], in_=stt[:, 12:16], func=AF.Exp, scale=-0.5),
                      r=[ks], w=[ks])
                yield 0.3
                for h in range(4):
                    p.dve(lambda h=h: nc.vector.tensor_scalar(
                        out=vn[:, h * 64:(h + 1) * 64], in0=vg[:, h * 64:(h + 1) * 64],
                        scalar1=stt[:, 8 + h:9 + h], scalar2=stt[:, 12 + h:13 + h], op0=ALU.subtract, op1=ALU.mult),
                        r=[kv, ks], w=[("B", 1)])
                yield 0.4
                rg = (tb % 2) * 256
                for h in range(4):
                    c, pb = h // 2, 64 * (h % 2)
                    p.pe(lambda h=h, c=c, pb=pb, rg=rg: nc.tensor.matmul(
                        PS[NB_][pb:pb + 64, rg + c * 128:rg + c * 128 + 128], lhsT=vn[:, h * 64:(h + 1) * 64],
                        rhs=T["wsT"][:, l * 4 + h, :], start=True, stop=True, skip_group_check=True),
                        r=[("B", 1), "wsT"], w=[("ps", NB_)])
                yield 0.3
                for c in range(2):
                    sl = slice(tb * 128, (tb + 1) * 128)
                    p.dve(lambda c=c, sl=sl, rg=rg: nc.vector.tensor_tensor(
                        out=F[5][:, c * 128:c * 128 + 128], in0=PS[NB_][:, rg + c * 128:rg + c * 128 + 128],
                        in1=T["bsb"][:, l * 2 + c, :], op=ALU.add),
                        r=[("ps", NB_), "bsb"], w=[("F", 5)])
                    p.pool(lambda c=c, sl=sl: nc.gpsimd.tensor_tensor(
                        out=yT[:, 4 + c, sl], in0=F[5][:, c * 128:c * 128 + 128], in1=ug[c][:, sl], op=ALU.mult),
                        r=[("F", 5), ("F", 2 + c)], w=[("yT", 4 + c)])
                yield 0.4

        def g_lru(c):
            XB, XC, R, I = F[2], F[3], F[4], F[5]
            kXB, kXC, kR, kI = ("F", 2), ("F", 3), ("F", 4), ("F", 5)
            A_, TH = F[0], F[1]
            kA, kTH = ("F", 0), ("F", 1)
            hidx = l * 2 + c
            p.dve(lambda: nc.vector.tensor_copy(out=XB[:, 0:3], in_=T["lruh"][:, hidx, :]),
                  r=[("lruh", hidx)], w=[kXB])
            self.proj_fm(l, C0["lx"] + c * 128)
            yield 2.1
            p.dve(lambda: nc.vector.tensor_copy(out=XB[:, 3:3 + TT], in_=PS[OB][:, :]),
                  r=[("ps", OB)], w=[kXB])
            yield 0.5
            p.dve(lambda: nc.vector.tensor_copy(out=T["lruh"][:, hidx, :], in_=XB[:, TT:TT + 3]),
                  r=[kXB], w=[("lruh", hidx)])
            cw = soff[("convw", l)] + c * 4
            cb = soff[("convb", l)] + c
            p.dve(lambda: nc.vector.tensor_scalar(out=XC[:, 0:TT], in0=XB[:, 3:3 + TT],
                                                  scalar1=small[:, cw + 3:cw + 4], scalar2=small[:, cb:cb + 1],
                                                  op0=ALU.mult, op1=ALU.add),
                  r=[kXB, "small"], w=[kXC])
            yield 0.5
            for j in range(3):
                p.dve(lambda j=j: nc.vector.scalar_tensor_tensor(
                    out=XC[:, 0:TT], in0=XB[:, j:j + TT], scalar=small[:, cw + j:cw + j + 1], op0=ALU.mult,
                    in1=XC[:, 0:TT], op1=ALU.add),
                    r=[kXB, kXC, "small"], w=[kXC])
                yield 0.5
            xcb = B[0]
            p.pool(lambda: nc.gpsimd.tensor_copy(out=xcb[:, :], in_=XC[:, 0:TT]), r=[kXC], w=[("B", 0)])
            yield 0.5
            nbo = l * 4 + c
            for (dst, kd, wi_, bo) in ((R, kR, 2, nbo), (I, kI, 4, nbo + 2)):
                p.pe(lambda wi_=wi_: nc.tensor.matmul(
                    PS[OB][:, :], lhsT=T["bd"][:, l * 6 + wi_ + c, :], rhs=xcb[:, :], start=True, stop=True),
                    r=["bd", ("B", 0)], w=[("ps", OB)])
                yield 0.3
                p.act(lambda dst=dst, bo=bo: nc.scalar.activation(
                    out=dst[:, 0:TT], in_=PS[OB][:, :], func=AF.Exp, bias=T["nb"][:, bo:bo + 1], scale=-1.0),
                    r=[("ps", OB), "nb"], w=[kd])
                yield 0.5
                p.pool(lambda dst=dst: nc.gpsimd.tensor_scalar(out=dst[:, 0:TT], in0=dst[:, 0:TT], scalar1=1.0,
                                                               scalar2=1.0, op0=ALU.mult, op1=ALU.add),
                       r=[kd], w=[kd])
                yield 0.5
                p.dve(lambda dst=dst: nc.vector.reciprocal(out=dst[:, 0:TT], in_=dst[:, 0:TT]), r=[kd], w=[kd])
                yield 1.5
            p.act(lambda: nc.scalar.activation(out=A_[:, 0:TT], in_=R[:, 0:TT], func=AF.Exp,
                                               scale=T["cneg"][:, l * 2 + c:l * 2 + c + 1]),
                  r=[kR, "cneg"], w=[kA])
            p.act(lambda: nc.scalar.activation(out=TH[:, 0:TT], in_=R[:, 0:TT], func=AF.Tanh,
                                               scale=T["cneg"][:, 2 * L + l * 2 + c:2 * L + l * 2 + c + 1]),
                  r=[kR, "cneg"], w=[kTH])
            yield 1.0
            p.pool(lambda: nc.gpsimd.tensor_tensor(out=I[:, 0:TT], in0=I[:, 0:TT], in1=XC[:, 0:TT], op=ALU.mult),
                   r=[kI, kXC], w=[kI])
            yield 0.5
            p.pool(lambda: nc.gpsimd.tensor_tensor(out=R[:, 0:TT], in0=A_[:, 0:TT], in1=A_[:, 0:TT], op=ALU.mult),
                   r=[kA, kTH], w=[kR])
            yield 0.5
            p.dve(lambda: nc.vector.scalar_tensor_tensor(out=R[:, 0:TT], in0=R[:, 0:TT], scalar=1.0, op0=ALU.add,
                                                         in1=TH[:, 0:TT], op1=ALU.mult),
                  r=[kR, kTH], w=[kR])
            p.dve(lambda: nc.vector.tensor_scalar(out=R[:, 0:TT], in0=R[:, 0:TT], scalar1=1e-30, scalar2=None,
                                                  op0=ALU.max), r=[kR], w=[kR])
            yield 1.0
            p.act(lambda: nc.scalar.activation(out=R[:, 0:TT], in_=R[:, 0:TT], func=AF.Ln), r=[kR], w=[kR])
            p.act(lambda: nc.scalar.activation(out=R[:, 0:TT], in_=R[:, 0:TT], func=AF.Exp, scale=0.5),
                  r=[kR], w=[kR])
            yield 1.0
            p.dve(lambda: nc.vector.tensor_tensor(out=I[:, 0:TT], in0=I[:, 0:TT], in1=R[:, 0:TT], op=ALU.mult),
                  r=[kI, kR], w=[kI])
            yield 0.5
            p.dve(lambda: nc.vector.tensor_tensor_scan(
                out=XC[:, 0:TT], data0=A_[:, 0:TT], data1=I[:, 0:TT], initial=T["hst"][:, hidx:hidx + 1],
                op0=ALU.mult, op1=ALU.add),
                r=[kA, kI, ("hst", hidx)], w=[kXC])
            p.dve(lambda: nc.vector.tensor_copy(out=T["hst"][:, hidx:hidx + 1], in_=XC[:, TT - 1:TT]),
                  r=[kXC], w=[("hst", hidx)])
            yield 1.0
            self.proj_fm(l, C0["lg"] + c * 128)
            yield 2.1
            yield from self.gelu(lambda: F[4][:, 0:TT], [("F", 4)], lambda: PS[OB][:, :], [("ps", OB)], 0, 1, TT)
            p.dve(lambda: nc.vector.tensor_tensor(out=yT[:, 6 + c, :], in0=XC[:, 0:TT], in1=F[4][:, 0:TT],
                                                  op=ALU.mult),
                  r=[kXC, ("F", 4)], w=[("yT", 6 + c)])
            yield 0.5

        def g_acd():
            yield from g_pool(0)
            yield from g_pool(1)
            yield from g_sgu()
            yield from g_lru(0)
            yield from g_lru(1)

        ga, gb = g_attn(), g_acd()
        va = vb = 0.0
        alive_a = alive_b = True
        while alive_a or alive_b:
            if alive_a and (va <= vb or not alive_b):
                try:
                    c_ = next(ga)
                    va += c_
                    if va > vb or not alive_b:
                        yield c_
                except StopIteration:
                    alive_a = False
            else:
                try:
                    c_ = next(gb)
                    vb += c_
                    if vb > va or not alive_a:
                        yield c_
                except StopIteration:
                    alive_b = False

        for dc in range(8):
            for kc in range(8):
                wap, wk = ws.get("mix_w_out", l, kc * 128, dc * 128)
                p.pe(lambda wap=wap, kc=kc: nc.tensor.matmul(
                    PS[OB][:, :], lhsT=wap, rhs=yT[:, kc, :], start=(kc == 0), stop=(kc == 7)),
                    r=[wk, ("yT", kc)], w=[("ps", OB)])
            p.dve(lambda dc=dc: nc.vector.tensor_tensor(
                out=xT[:, dc, :], in0=PS[OB][:, :], in1=xT[:, dc, :], op=ALU.add),
                r=[("ps", OB), ("xT", XP, dc)], w=[("xT", XP, dc)])
            yield 2.1
        ws.end_phase()


_CACHE = {}


def _get_prog(S, L, final):
    key = (S, L, final)
    if key not in _CACHE:
        _CACHE[key] = build(S, L, final)
    return _CACHE[key]


def _pack_weights(ws, W, lsel):
    CH = ws.CH
    wst = np.zeros((ws.total_chunks, 128, CH), np.float32)
    for key, ph in ws.phases.items():
        base = ph["base"]
        for (name, l, r0, c0, ncols, pos) in ph["blocks"]:
            ch, off = base + pos // CH, pos % CH
            wst[ch, :, off:off + ncols] = W[name][lsel[l], r0:r0 + 128, c0:c0 + ncols]
    return wst


def _run(xT_list, W, lsel, final):
    S = xT_list[0].shape[1]
    L = len(lsel)
    nc, packs, NS, soff = _get_prog(S, L, final)
    wstF = _pack_weights(packs["F"], W, lsel)
    wstM = _pack_weights(packs["M"], W, lsel)
    small = np.zeros((128, NS), np.float32)
    for li, l in enumerate(lsel):
        small[:, soff[("g1", li)]:soff[("g1", li)] + 8] = _vec8(W["ffn1_norm"][l])
        small[:, soff[("gm", li)]:soff[("gm", li)] + 8] = _vec8(W["mix_norm"][l])
        small[:, soff[("g2", li)]:soff[("g2", li)] + 8] = _vec8(W["ffn2_norm"][l])
        small[:, soff[("pscale", li)]:soff[("pscale", li)] + 2] = _vec2(W["pool_scale"][l])
        cw = W["conv_w"][l]
        for c in range(2):
            for j in range(4):
                small[:, soff[("convw", li)] + c * 4 + j] = cw[j, c * 128:(c + 1) * 128]
        small[:, soff[("convb", li)]:soff[("convb", li)] + 2] = _vec2(W["conv_b"][l])
        small[:, soff[("ba", li)]:soff[("ba", li)] + 2] = _vec2(W["lru_ba"][l])
        small[:, soff[("bx", li)]:soff[("bx", li)] + 2] = _vec2(W["lru_bx"][l])
        small[:, soff[("lam", li)]:soff[("lam", li)] + 2] = _vec2(W["lru_lambda"][l])
    if final:
        small[:, soff[("gf", 0)]:soff[("gf", 0)] + 8] = _vec8(W["final_norm"])
    bd = np.zeros((L, 6, 128, 128), np.float32)
    wsT = np.zeros((L, 4, 128, 128), np.float32)
    bsb = np.zeros((L, 2, 128, 128), np.float32)
    for li, l in enumerate(lsel):
        for c in range(2):
            bd[li, c] = _blockdiag(W["pool_w"][l, 2 * c:2 * c + 2])
            bd[li, 2 + c] = _blockdiag(W["lru_wa"][l, 2 * c:2 * c + 2])
            bd[li, 4 + c] = _blockdiag(W["lru_wx"][l, 2 * c:2 * c + 2])
            for hf in range(2):
                bsb[li, c, 64 * hf:64 * hf + 64, :] = W["sgu_b"][l, 2 * c + hf][None, :]
        for h in range(4):
            wsT[li, h] = W["sgu_w"][l, h].T
    cst = _consts()
    shared = {"wstF": wstF, "wstM": wstM, "small": small, "cst": cst, "bd": bd, "wsT": wsT, "bsb": bsb}
    in_maps = [dict(shared, xT=np.ascontiguousarray(x)) for x in xT_list]
    n = len(xT_list)
    res = run_bass_kernel_spmd(nc, in_maps, core_ids=list(range(n)))
    return [r["yT"] for r in res.results]


def kernel(**inputs):
    W = {k: np.asarray(v, dtype=np.float32) for k, v in inputs.items()}
    x = W.pop("x")
    Bn = x.shape[0]
    xT = [np.ascontiguousarray(x[b].T) for b in range(Bn)]
    depth = W["ffn1_norm"].shape[0]
    outs = _run(xT, W, list(range(depth)), True)
    return np.stack([o.T for o in outs], axis=0).astype(np.float32)
```

```python
import contextlib
import numpy as np
import concourse.bass as bass
import concourse.mybir as mybir
from concourse.bass_utils import run_bass_kernel_spmd

F32 = mybir.dt.float32
BF16 = mybir.dt.bfloat16
AF = mybir.ActivationFunctionType
ALU = mybir.AluOpType
AX = mybir.AxisListType

D = 1024
DFF = 2816
NFC = DFF // 128
TT = 512


EPS = 1e-6
GELU_C = 0.7978845608028654


class _Op:
    __slots__ = ("eng", "fn", "dma", "deps", "ms", "dsem", "dval", "idx", "has_dep", "sw", "gen", "swk")

    def __init__(self, eng, fn, dma):
        self.eng = eng
        self.fn = fn
        self.dma = dma
        self.deps = ()
        self.ms = 0
        self.dsem = None
        self.dval = 0
        self.has_dep = False
        self.sw = None
        self.gen = 0
        self.swk = 0


class Prog:
    ENGS = ("pe", "act", "dve", "pool", "sp")

    def __init__(self, nc, n_dma_sems=20, same_engine_sync=True):
        self.nc = nc
        self.ops = {e: [] for e in self.ENGS}
        self.last_w = {}
        self.readers = {}
        self.n_dma_sems = n_dma_sems
        self.same_engine_sync = same_engine_sync
        self.dma_rr = 0
        self.dma_last = [None] * n_dma_sems
        self.dma_cnt = [0] * n_dma_sems
        self.sw_gen = {}
        self.sw_count = 0

    def add(self, eng, fn, reads=(), writes=(), dma=False, sw=None):
        op = _Op(eng, fn, dma)
        op.idx = len(self.ops[eng])
        self.ops[eng].append(op)
        deps = set()
        for k in reads:
            w = self.last_w.get(k)
            if w is not None:
                deps.add(w)
        for k in writes:
            w = self.last_w.get(k)
            if w is not None:
                deps.add(w)
            rd = self.readers.get(k)
            if rd:
                deps.update(rd.values())
        for k in reads:
            self.readers.setdefault(k, {})[eng] = op
        for k in writes:
            self.last_w[k] = op
            self.readers[k] = {}
        if sw is not None:
            op.sw = sw
            op.gen = self.sw_gen.get(sw, 0)
            self.sw_gen[sw] = op.gen + 1
            self.sw_count += 1
            op.swk = self.sw_count
        elif dma:
            s = self.dma_rr
            self.dma_rr = (self.dma_rr + 1) % self.n_dma_sems
            prev = self.dma_last[s]
            if prev is not None:
                deps.add(prev)
            self.dma_cnt[s] += 1
            op.dsem = s
            op.dval = 16 * self.dma_cnt[s]
            self.dma_last[s] = op
        deps.discard(op)
        op.deps = deps
        for d in deps:
            d.has_dep = True
        return op

    def pe(self, fn, r=(), w=()):
        return self.add("pe", fn, r, w)

    def act(self, fn, r=(), w=()):
        return self.add("act", fn, r, w)

    def dve(self, fn, r=(), w=()):
        return self.add("dve", fn, r, w)

    def pool(self, fn, r=(), w=()):
        return self.add("pool", fn, r, w)

    def dma(self, q, fn, r=(), w=()):
        return self.add(q, fn, r, w, dma=True)

    def dma_sw(self, slot, fn, r=(), w=()):
        return self.add("pool", fn, r, w, dma=True, sw=slot)

    def emit(self):
        nc = self.nc
        for e in self.ENGS:
            c = 0
            for op in self.ops[e]:
                if op.has_dep and not op.dma:
                    c += 1
                    op.ms = c
        with contextlib.ExitStack() as st:
            esem = {e: st.enter_context(nc.semaphore("ms_" + e)) for e in self.ENGS}
            dsem = [st.enter_context(nc.semaphore("dq%d" % i)) for i in range(self.n_dma_sems)]
            sw_unique = self.sw_count <= 100
            if sw_unique:
                swsem = {k: st.enter_context(nc.semaphore("sw%d" % k)) for k in range(1, self.sw_count + 1)}
            else:
                swsem = {k: st.enter_context(nc.semaphore("sw%s" % k)) for k in sorted(self.sw_gen)}
            block = st.enter_context(nc.Block())
            engobj = {"pe": nc.tensor, "act": nc.scalar, "dve": nc.vector,
                      "pool": nc.gpsimd, "sp": nc.sync}
            self.n_waits = 0

            def body(e):
                eng = engobj[e]
                waited = {}
                for op in self.ops[e]:
                    need = {}
                    for d in op.deps:
                        if d.sw is not None:
                            if sw_unique:
                                if not waited.get(("swu", d.swk)):
                                    waited[("swu", d.swk)] = 1
                                    eng.wait_ge(swsem[d.swk], 16)
                                    self.n_waits += 1
                            elif waited.get(("sw", d.sw), 0) < d.gen + 1:
                                waited[("sw", d.sw)] = d.gen + 1
                                eng.wait_ge(swsem[d.sw], 16 * (d.gen + 1))
                                self.n_waits += 1
                            continue
                        if d.dma:
                            key = ("d", d.dsem)
                            val = d.dval
                        else:
                            if d.eng == e and (e == "pe" or not self.same_engine_sync):
                                continue
                            key = ("e", d.eng)
                            val = d.ms
                        if need.get(key, 0) < val:
                            need[key] = val
                    for key, val in need.items():
                        if waited.get(key, 0) >= val:
                            continue
                        waited[key] = val
                        sem = dsem[key[1]] if key[0] == "d" else esem[key[1]]
                        eng.wait_ge(sem, val)
                        self.n_waits += 1
                    if op.sw is not None:
                        op.fn().then_inc(swsem[op.swk if sw_unique else op.sw], 16)
                        continue
                    inst = op.fn()
                    if op.dma:
                        inst.then_inc(dsem[op.dsem], 16)
                    elif op.ms:
                        inst.then_inc(esem[e], 1)

            @block.tensor
            def _(t):
                body("pe")

            @block.scalar
            def _(t):
                body("act")

            @block.vector
            def _(t):
                body("dve")

            @block.gpsimd
            def _(t):
                body("pool")

            @block.sync
            def _(t):
                body("sp")


class WStream:
    def __init__(self, tag, ch, nslot):
        self.tag = tag
        self.CH = ch
        self.NSLOT = nslot
        self.phases = {}
        self.total_chunks = 0
        self.dry = True
        self.cur_key = None
        self.pos = 0

    def begin_phase(self, key):
        self.cur_key = key
        self.pos = 0
        if self.dry:
            self.recording = key not in self.phases
            if self.recording:
                self.phases[key] = {"blocks": [], "nch": 0, "base": self.total_chunks}
        else:
            ph = self.phases[key]
            assert self.order[self.oi] == key, (self.order[self.oi], key)
            self.gbase = self.obase[self.oi]
            self.oi += 1

    def end_phase(self):
        if self.dry and self.recording:
            ph = self.phases[self.cur_key]
            ph["nch"] = (self.pos + self.CH - 1) // self.CH
            self.total_chunks += ph["nch"]

    def arm(self, p, nc, wst, ring, order):
        self.dry = False
        self.p, self.nc, self.wst, self.ring = p, nc, wst, ring
        self.order = order
        self.oi = 0
        self.G = []
        self.obase = []
        for key in order:
            ph = self.phases[key]
            self.obase.append(len(self.G))
            self.G.extend(range(ph["base"], ph["base"] + ph["nch"]))
        self.cur = -1
        self.issued = -1

    def _issue(self, g):
        if g >= len(self.G) or g <= self.issued:
            return
        assert g == self.issued + 1
        self.issued = g
        nc, ring, wst = self.nc, self.ring, self.wst
        slot = g % self.NSLOT
        c = self.G[g]
        a = self.CH // 1024
        self.p.dma_sw(self.tag + str(slot),
                      lambda: nc.gpsimd.dma_start(
                          out=ring[slot][:, :].rearrange("p (a b) -> p a b", a=a),
                          in_=wst[c].rearrange("p (a b) -> p a b", a=a)),
                      w=[("ring" + self.tag, slot)])

    def get(self, name, l, r0, c0, ncols=128, reserve=0):
        CH = self.CH
        if (self.pos % CH) + max(ncols, reserve) > CH:
            self.pos = (self.pos // CH + 1) * CH
        if self.dry:
            if self.recording:
                self.phases[self.cur_key]["blocks"].append((name, l, r0, c0, ncols, self.pos))
            self.pos += ncols
            return None, ("ring" + self.tag, 0)
        chunk = self.pos // CH
        off = self.pos % CH
        self.pos += ncols
        g = self.gbase + chunk
        if g != self.cur:
            if self.cur < 0:
                for i in range(self.NSLOT):
                    self._issue(i)
            else:
                assert g == self.cur + 1, (g, self.cur)
                self._issue(g - 1 + self.NSLOT)
            self.cur = g
        slot = g % self.NSLOT
        return self.ring[slot][:, off:off + ncols], ("ring" + self.tag, slot)


def _small_layout(L, final):
    off = {}
    n = 0

    def put(name, w):
        nonlocal n
        off[name] = n
        n += w
    for l in range(L):
        put(("g1", l), 8)
        put(("gm", l), 8)
        put(("g2", l), 8)
        put(("pscale", l), 2)
        put(("convw", l), 8)
        put(("convb", l), 2)
        put(("ba", l), 2)
        put(("bx", l), 2)
        put(("lam", l), 2)
    if final:
        put(("gf", 0), 8)
    return off, n


def _vec2(v):
    return np.ascontiguousarray(v.reshape(2, 128).T)


def _vec8(v):
    return np.ascontiguousarray(v.reshape(8, 128).T)


def _blockdiag(w2):
    o = np.zeros((128, 128), np.float32)
    o[0:64, 0:64] = w2[0]
    o[64:128, 64:128] = w2[1]
    return o


def _consts():
    j = np.arange(128)[:, None]
    s = np.arange(128)[None, :]
    ones = np.ones((128, 128), np.float32)
    onesneg = -ones
    trineg = -(j >= s).astype(np.float32)
    ident = np.eye(128, dtype=np.float32)
    negmask = np.where(j >= s, -30000.0, 0.0).astype(np.float32)
    return np.concatenate([ones, onesneg, trineg, ident, negmask], axis=1)


CHF, NSF = 3072, 3
CHM, NSM = 2048, 3
FGROUPS = ((0, 8), (8, 15), (15, 22))


def _merge(gens):
    vt = [0.0] * len(gens)
    alive = [True] * len(gens)
    while any(alive):
        i = min((j for j in range(len(gens)) if alive[j]), key=lambda j: vt[j])
        try:
            c = next(gens[i])
            vt[i] += (c or 0.1)
        except StopIteration:
            alive[i] = False


def _chain(*gens):
    for g in gens:
        yield from g


class _Z:
    def __getitem__(self, k):
        return self


def _schedule(NT, L):
    if NT < 2 or NT % 2:
        steps = []
        for t in range(NT):
            steps.append([[("F", t, 0, 1)]])
            for l in range(L):
                steps.append([[("M", t, l)]])
                nxt = [("F", t, l, 2)]
                nxt.append(("F", t, l + 1, 1) if l + 1 < L else ("FIN", t))
                steps.append([nxt])
            if t + 2 < NT:
                steps.append([[("LD", t + 2)]])
        return steps
    steps = [[[("F", 0, 0, 1)]]]
    carry = []
    for a in range(0, NT, 2):
        b, c, d = a + 1, a + 2, a + 3
        for l in range(L):
            if l == 0:
                fa = carry + [("F", b, 0, 1)]
            else:
                fa = [("F", b, l - 1, 2), ("F", b, l, 1)]
            steps.append([[("M", a, l)], fa])
            fb = [("F", a, l, 2)]
            if l + 1 < L:
                fb.append(("F", a, l + 1, 1))
            else:
                fb.append(("FIN", a))
                if c < NT:
                    fb += [("LD", c), ("F", c, 0, 1)]
            steps.append([[("M", b, l)], fb])
        carry = [("F", b, L - 1, 2), ("FIN", b)]
        if d < NT:
            carry.append(("LD", d))
    steps.append([carry])
    return steps


def build(S, L, final):
    NT = S // TT
    NKB = S // 128
    soff, NS = _small_layout(L, final)
    steps = _schedule(NT, L)

    wsF = WStream("F", CHF, NSF)
    wsM = WStream("M", CHM, NSM)
    G = _Gen(None, Prog(None), wsF, wsM, L, final, S, None, soff)
    for l in range(L):
        for which in (1, 2):
            for _ in G.g_ffn(0, l, which):
                pass
        for _ in G.g_mixer(0, l):
            pass
    orderF, orderM = [], []
    for st in steps:
        for stream in st:
            for ph in stream:
                if ph[0] == "F":
                    orderF.append(("F", ph[2], ph[3]))
                elif ph[0] == "M":
                    orderM.append(("M", ph[2]))

    nc = bass.Bass("TRN2", target_bir_lowering=False)
    x_in = nc.dram_tensor("xT", [D, S], F32, kind="ExternalInput").ap()
    wstF = nc.dram_tensor("wstF", [wsF.total_chunks, 128, CHF], F32, kind="ExternalInput").ap()
    wstM = nc.dram_tensor("wstM", [wsM.total_chunks, 128, CHM], F32, kind="ExternalInput").ap()
    small_in = nc.dram_tensor("small", [128, NS], F32, kind="ExternalInput").ap()
    cst_in = nc.dram_tensor("cst", [128, 5 * 128], F32, kind="ExternalInput").ap()
    bd_in = nc.dram_tensor("bd", [L, 6, 128, 128], F32, kind="ExternalInput").ap()
    ws_in = nc.dram_tensor("wsT", [L, 4, 128, 128], F32, kind="ExternalInput").ap()
    bs_in = nc.dram_tensor("bsb", [L, 2, 128, 128], F32, kind="ExternalInput").ap()
    y_out = nc.dram_tensor("yT", [D, S], F32, kind="ExternalOutput").ap()

    with contextlib.ExitStack() as st:
        def sb(name, shape, dt):
            return st.enter_context(nc.sbuf_tensor(name, shape, dt))

        T = {}
        T["xT"] = [sb("xT_sb%d" % i, [128, 8, TT], F32) for i in range(2)]
        T["xnF"] = sb("xnF", [128, 8, TT], BF16)
        T["aT"] = sb("aT", [128, 8, TT], BF16)
        T["Fs"] = [sb("Fs%d" % i, [128, TT], F32) for i in range(2)]
        T["xnM"] = sb("xnM", [128, 8, TT], BF16)
        T["yT"] = sb("yTm", [128, 8, TT], BF16)
        T["kc"] = sb("kc", [128, L * 2, S], BF16)
        T["vc"] = sb("vc", [128, L * NKB, 256], BF16)
        T["ringF"] = [sb("ringF%d" % i, [128, CHF], BF16) for i in range(NSF)]
        T["ringM"] = [sb("ringM%d" % i, [128, CHM], BF16) for i in range(NSM)]
        T["small"] = sb("small_sb", [128, NS], F32)
        T["cst"] = sb("cstb", [128, 5 * 128], BF16)
        T["bd"] = sb("bdb", [128, L * 6, 128], BF16)
        T["wsT"] = sb("wsTb", [128, L * 4, 128], BF16)
        T["bsb"] = sb("bsb_sb", [128, L * 2, 128], F32)
        T["cneg"] = sb("cneg", [128, L * 4], F32)
        T["nb"] = sb("nb", [128, L * 4], F32)
        T["epsb"] = sb("epsb", [128, 1], F32)
        T["icnt"] = sb("icnt", [128, 2, 16], F32)
        T["poolh"] = sb("poolh", [128, L * 2, 16], F32)
        T["lruh"] = sb("lruh", [128, L * 2, 3], F32)
        T["hst"] = sb("hst", [128, L * 2], F32)
        T["F"] = [sb("F%d" % i, [128, TT + 16], F32) for i in range(7)]
        T["B"] = [sb("B%d" % i, [128, TT], BF16) for i in range(6)]
        T["S"] = [sb("S%d" % i, [128, TT], BF16) for i in range(2)]
        T["S32"] = [sb("S32_%d" % i, [128, TT], F32) for i in range(2)]
        T["E"] = [sb("E%d" % i, [128, TT], F32) for i in range(2)]
        T["qz"] = [sb("qz%d" % i, [128, TT], BF16) for i in range(4)]
        T["st"] = sb("stat", [128, 32], F32)
        T["PS"] = [st.enter_context(nc.psum_tensor("ps%d" % i, [128, 512], F32)) for i in range(8)]
        T["x_in"] = x_in
        T["y_out"] = y_out

        p = Prog(nc)
        wsF.arm(p, nc, wstF, T["ringF"], orderF)
        wsM.arm(p, nc, wstM, T["ringM"], orderM)
        F = T["F"]

        p.dma("sp", lambda: nc.sync.dma_start(out=T["small"][:, :], in_=small_in[:, :]), w=["small"])
        for i in range(5):
            stg = F[i % 2]
            p.dma("sp", lambda i=i, stg=stg: nc.sync.dma_start(out=stg[:, 0:128], in_=cst_in[:, i * 128:(i + 1) * 128]),
                  w=[("F", i % 2)])
            p.dve(lambda i=i, stg=stg: nc.vector.tensor_copy(out=T["cst"][:, i * 128:(i + 1) * 128], in_=stg[:, 0:128]),
                  r=[("F", i % 2)], w=["cst"])
        k_ = 0
        for l in range(L):
            for i in range(6):
                stg = F[k_ % 2]
                p.dma("sp", lambda l=l, i=i, stg=stg: nc.sync.dma_start(out=stg[:, 0:128], in_=bd_in[l, i]),
                      w=[("F", k_ % 2)])
                p.dve(lambda l=l, i=i, stg=stg: nc.vector.tensor_copy(out=T["bd"][:, l * 6 + i, :], in_=stg[:, 0:128]),
                      r=[("F", k_ % 2)], w=["bd"])
                k_ += 1
            for h in range(4):
                stg = F[k_ % 2]
                p.dma("sp", lambda l=l, h=h, stg=stg: nc.sync.dma_start(out=stg[:, 0:128], in_=ws_in[l, h]),
                      w=[("F", k_ % 2)])
                p.pool(lambda l=l, h=h, stg=stg: nc.gpsimd.affine_select(
                    out=T["wsT"][:, l * 4 + h, :], in_=stg[:, 0:128], pattern=[[1, 128]],
                    compare_op=ALU.is_ge, fill=0.0, base=0, channel_multiplier=-1),
                    r=[("F", k_ % 2)], w=["wsT"])
                k_ += 1
            for c in range(2):
                p.dma("sp", lambda l=l, c=c: nc.sync.dma_start(out=T["bsb"][:, l * 2 + c, :], in_=bs_in[l, c]),
                      w=["bsb"])
            lo = soff[("lam", l)]
            p.act(lambda l=l, lo=lo: nc.scalar.activation(out=T["st"][:, 0:2], in_=T["small"][:, lo:lo + 2],
                                                           func=AF.Exp, scale=-1.0),
                  r=["small"], w=["st"])
            p.act(lambda l=l: nc.scalar.activation(out=T["st"][:, 2:4], in_=T["st"][:, 0:2], func=AF.Ln,
                                                    bias=1.0, scale=1.0),
                  r=["st"], w=["st"])
            p.dve(lambda l=l: nc.vector.tensor_scalar(out=T["cneg"][:, 2 * l:2 * l + 2], in0=T["st"][:, 2:4],
                                                      scalar1=-8.0, scalar2=None, op0=ALU.mult),
                  r=["st"], w=["cneg"])
            p.dve(lambda l=l: nc.vector.tensor_scalar(out=T["cneg"][:, 2 * L + 2 * l:2 * L + 2 * l + 2],
                                                      in0=T["st"][:, 2:4], scalar1=8.0, scalar2=None, op0=ALU.mult),
                  r=["st"], w=["cneg"])
        p.dve(lambda: nc.vector.memset(T["epsb"][:, :], EPS), w=["epsb"])
        for l in range(L):
            for k2_, nm in enumerate(("ba", "bx")):
                so = soff[(nm, l)]
                p.dve(lambda l=l, k2_=k2_, so=so: nc.vector.tensor_scalar(
                    out=T["nb"][:, l * 4 + 2 * k2_:l * 4 + 2 * k2_ + 2], in0=T["small"][:, so:so + 2],
                    scalar1=-1.0, scalar2=None, op0=ALU.mult), r=["small"], w=["nb"])
        wins = (2, 4, 8, 16)
        for c in range(2):
            for hf in range(2):
                g = 2 * c + hf
                p.dve(lambda c=c, hf=hf, g=g: nc.vector.memset(T["icnt"][64 * hf:64 * hf + 64, c, :], 1.0 / wins[g]),
                      w=["icnt"])
        for t in range(15):
            for c in range(2):
                for hf in range(2):
                    g = 2 * c + hf
                    if wins[g] > t + 1:
                        p.dve(lambda c=c, hf=hf, t=t: nc.vector.memset(
                            T["icnt"][64 * hf:64 * hf + 64, c, t:t + 1], 1.0 / (t + 1)), w=["icnt"])
        p.dve(lambda: nc.vector.memset(T["poolh"][:, :, :], 0.0), w=["poolh"])
        p.dve(lambda: nc.vector.memset(T["lruh"][:, :, :], 0.0), w=["lruh"])
        p.dve(lambda: nc.vector.memset(T["hst"][:, :], 0.0), w=["hst"])
        for h in range(4):
            p.dve(lambda h=h: nc.vector.memset(T["qz"][h][:, :], 0.0), w=[("qz", h)])

        G = _Gen(nc, p, wsF, wsM, L, final, S, T, soff)
        G.load_x(0)
        if NT > 1:
            G.load_x(1)

        def _ld(t_):
            G.load_x(t_)
            yield 0.1
        for stp in steps:
            gens = []
            for stream in stp:
                parts = []
                for ph in stream:
                    if ph[0] == "F":
                        parts.append(G.g_ffn(ph[1], ph[2], ph[3]))
                    elif ph[0] == "M":
                        parts.append(G.g_mixer(ph[1], ph[2]))
                    elif ph[0] == "LD":
                        parts.append(_ld(ph[1]))
                    else:
                        parts.append(G.g_final(ph[1]))
                gens.append(_chain(*parts))
            _merge(gens)
        p.add("sp", lambda: nc.sync.nop(), reads=["y_hbm"])
        p.emit()
    packs = {"F": wsF, "M": wsM}
    return nc, packs, NS, soff


class _Gen:
    def __init__(self, nc, p, wsF, wsM, L, final, S, T, soff):
        self.nc, self.p, self.wsF, self.wsM = nc, p, wsF, wsM
        self.L, self.final, self.S, self.soff = L, final, S, soff
        if T is None:
            T = {k: _Z() for k in ("xT", "xnF", "xnM", "aT", "yT", "kc", "vc", "small", "cst", "bd", "wsT", "bsb",
                                   "cneg", "nb", "icnt", "poolh", "lruh", "hst", "st", "x_in", "y_out",
                                   "epsb")}
            T.update({"F": [None] * 7, "B": [None] * 6, "S": [None] * 2, "S32": [None] * 2, "E": [None] * 2,
                      "PS": [None] * 8, "Fs": [None] * 2, "qz": [None] * 4})
        self.T = T
        self.fbank = 0
        self.zrr = 0

    def load_x(self, t_):
        nc, p, T = self.nc, self.p, self.T
        buf = T["xT"][t_ % 2]
        p.dma("sp", lambda: nc.sync.dma_start(
            out=buf[:, :, :], in_=T["x_in"].rearrange("(c q) t -> q c t", q=128)[:, :, t_ * TT:t_ * TT + TT]),
            w=[("xT", t_ % 2, c) for c in range(8)])

    def _fb(self):
        v = (5, 6, 7)[self.fbank % 3]
        self.fbank += 1
        return v

    def rmsnorm(self, tt, gname, l, xn, xnk, bank, rbuf, rkey, out_fp32=False):
        nc, p, T = self.nc, self.p, self.T
        PS, small, cst = T["PS"], T["small"], T["cst"]
        xT = T["xT"][tt % 2]
        XP = tt % 2
        go = self.soff[(gname, l)]
        for c in range(8):
            if c % 2 == 0:
                p.act(lambda c=c: nc.scalar.activation(out=xn[:, c, :], in_=xT[:, c, :], func=AF.Square),
                      r=[("xT", XP, c)], w=[(xnk, c)])
            else:
                p.pool(lambda c=c: nc.gpsimd.tensor_tensor(out=xn[:, c, :], in0=xT[:, c, :], in1=xT[:, c, :],
                                                           op=ALU.mult),
                       r=[("xT", XP, c)], w=[(xnk, c)])
            p.pe(lambda c=c: nc.tensor.matmul(PS[bank][:, :], lhsT=cst[:, 0:128], rhs=xn[:, c, :],
                                              start=(c == 0), stop=(c == 7)),
                 r=[(xnk, c), "cst"], w=[("ps", bank)])
            yield 0.5
        p.act(lambda: nc.scalar.activation(out=rbuf[:, 0:TT], in_=PS[bank][:, :], func=AF.Ln,
                                           bias=T["epsb"][:, 0:1], scale=1.0 / D),
              r=[("ps", bank), "epsb"], w=[rkey])
        p.act(lambda: nc.scalar.activation(out=rbuf[:, 0:TT], in_=rbuf[:, 0:TT], func=AF.Exp, scale=-0.5),
              r=[rkey], w=[rkey])
        yield 1.0
        for c in range(8):
            dst = xT if out_fp32 else xn
            dk = ("xT", XP, c) if out_fp32 else (xnk, c)
            p.dve(lambda c=c, dst=dst: nc.vector.scalar_tensor_tensor(
                out=dst[:, c, :], in0=xT[:, c, :], scalar=small[:, go + c:go + c + 1], op0=ALU.mult,
                in1=rbuf[:, 0:TT], op1=ALU.mult),
                r=[("xT", XP, c), rkey, "small"], w=[dk])
            yield 0.5

    def g_ffn(self, tt, l, which):
        nc, p, T, ws = self.nc, self.p, self.T, self.wsF
        PS, xn, aT, Fs = T["PS"], T["xnF"], T["aT"], T["Fs"]
        xT = T["xT"][tt % 2]
        XP = tt % 2
        wi, wo = {1: ("ffn1_w_in", "ffn1_w_out"), 2: ("ffn2_w_in", "ffn2_w_out")}[which]
        ws.begin_phase(("F", l, which))
        yield from self.rmsnorm(tt, "g%d" % which, l, xn, "xnF", self._fb(), Fs[0], ("Fs", 0))
        for (f0, f1) in FGROUPS:
            for fc in range(f0, f1):
                j = fc - f0
                bg, bu = self._fb(), self._fb()
                for half, bank in ((0, bg), (1, bu)):
                    for kc in range(8):
                        wap, wk = ws.get(wi, l, kc * 128, half * DFF + fc * 128)
                        p.pe(lambda wap=wap, kc=kc, bank=bank: nc.tensor.matmul(
                            PS[bank][:, :], lhsT=wap, rhs=xn[:, kc, :], start=(kc == 0), stop=(kc == 7)),
                            r=[wk, ("xnF", kc)], w=[("ps", bank)])
                    yield 2.1
                f = Fs[j % 2]
                fk = ("Fs", j % 2)
                p.act(lambda bg=bg, f=f: nc.scalar.activation(out=f[:, :], in_=PS[bg][:, :], func=AF.Exp, scale=-1.0),
                      r=[("ps", bg)], w=[fk])
                p.act(lambda f=f: nc.scalar.activation(out=f[:, :], in_=f[:, :], func=AF.Ln, bias=1.0, scale=1.0),
                      r=[fk], w=[fk])
                p.act(lambda f=f: nc.scalar.activation(out=f[:, :], in_=f[:, :], func=AF.Exp, scale=-1.0),
                      r=[fk], w=[fk])
                p.dve(lambda bg=bg, f=f: nc.vector.tensor_tensor(out=f[:, :], in0=PS[bg][:, :], in1=f[:, :],
                                                                 op=ALU.mult),
                      r=[("ps", bg), fk], w=[fk])
                p.dve(lambda bu=bu, f=f, j=j: nc.vector.tensor_tensor(out=aT[:, j, :], in0=PS[bu][:, :],
                                                                       in1=f[:, :], op=ALU.mult),
                      r=[("ps", bu), fk], w=[("aT", j)])
            ng = f1 - f0
            for dc in range(8):
                bank = self._fb()
                for j in range(ng):
                    fc = f0 + j
                    wap, wk = ws.get(wo, l, fc * 128, dc * 128)
                    p.pe(lambda wap=wap, j=j, bank=bank, ng=ng: nc.tensor.matmul(
                        PS[bank][:, :], lhsT=wap, rhs=aT[:, j, :], start=(j == 0), stop=(j == ng - 1)),
                        r=[wk, ("aT", j)], w=[("ps", bank)])
                p.dve(lambda dc=dc, bank=bank: nc.vector.scalar_tensor_tensor(
                    out=xT[:, dc, :], in0=PS[bank][:, :], scalar=0.5, op0=ALU.mult, in1=xT[:, dc, :], op1=ALU.add),
                    r=[("ps", bank), ("xT", XP, dc)], w=[("xT", XP, dc)])
                yield 0.26 * ng
        ws.end_phase()

    def g_final(self, tt):
        nc, p, T = self.nc, self.p, self.T
        xT = T["xT"][tt % 2]
        XP = tt % 2
        t0 = tt * TT
        if self.final:
            yield from self.rmsnorm(tt, "gf", 0, T["xnF"], "xnF", self._fb(), T["Fs"][0], ("Fs", 0), out_fp32=True)
        p.dma("sp", lambda: nc.sync.dma_start(
            out=T["y_out"].rearrange("(c q) t -> q c t", q=128)[:, :, t0:t0 + TT], in_=xT[:, :, :]),
            r=[("xT", XP, c) for c in range(8)], w=["y_hbm"])
        yield 0.1

    def proj_fm(self, l, col0, bank=3):
        nc, p, T, ws = self.nc, self.p, self.T, self.wsM
        PS, xn = T["PS"], T["xnM"]
        for kc in range(8):
            wap, wk = ws.get("mix_w_in", l, kc * 128, col0)
            p.pe(lambda wap=wap, kc=kc: nc.tensor.matmul(
                PS[bank][:, :], lhsT=wap, rhs=xn[:, kc, :], start=(kc == 0), stop=(kc == 7)),
                r=[wk, ("xnM", kc)], w=[("ps", bank)])
        return bank

    def proj_tm_blocks(self, l, col0):
        return [self.wsM.get("mix_w_in", l, kc * 128, col0, 256, reserve=(8 - kc) * 256) for kc in range(8)]

    def proj_tm_tb(self, waps, tb, bank, half):
        nc, p, T = self.nc, self.p, self.T
        PS, xn = T["PS"], T["xnM"]
        for kc in range(8):
            wap, wk = waps[kc]
            p.pe(lambda wap=wap, kc=kc: nc.tensor.matmul(
                PS[bank][:, half * 256:half * 256 + 256], lhsT=xn[:, kc, tb * 128:(tb + 1) * 128], rhs=wap,
                start=(kc == 0), stop=(kc == 7), skip_group_check=True),
                r=[wk, ("xnM", kc)], w=[("ps", bank)])

    def gelu(self, dst_fn, dkeys, src_fn, skeys, ia, ib, width):
        nc, p, T = self.nc, self.p, self.T
        F = T["F"]
        fa, fb = F[ia], F[ib]
        ka, kb = ("F", ia), ("F", ib)
        w = width
        cst_ = w / 1000.0
        p.dve(lambda: nc.vector.tensor_copy(out=fa[:, 0:w], in_=src_fn()), r=skeys, w=[ka])
        yield cst_
        p.pool(lambda: nc.gpsimd.tensor_tensor(out=fb[:, 0:w], in0=fa[:, 0:w], in1=fa[:, 0:w], op=ALU.mult),
               r=[ka], w=[kb])
        yield cst_
        p.dve(lambda: nc.vector.tensor_scalar(out=fb[:, 0:w], in0=fb[:, 0:w], scalar1=0.044715, scalar2=1.0,
                                              op0=ALU.mult, op1=ALU.add), r=[kb], w=[kb])
        yield cst_
        p.pool(lambda: nc.gpsimd.tensor_tensor(out=fb[:, 0:w], in0=fb[:, 0:w], in1=fa[:, 0:w], op=ALU.mult),
               r=[ka, kb], w=[kb])
        yield cst_
        p.act(lambda: nc.scalar.activation(out=fb[:, 0:w], in_=fb[:, 0:w], func=AF.Exp, scale=-2.0 * GELU_C),
              r=[kb], w=[kb])
        yield cst_
        p.pool(lambda: nc.gpsimd.tensor_scalar(out=fb[:, 0:w], in0=fb[:, 0:w], scalar1=1.0, scalar2=1.0,
                                               op0=ALU.mult, op1=ALU.add), r=[kb], w=[kb])
        yield cst_
        p.dve(lambda: nc.vector.reciprocal(out=fb[:, 0:w], in_=fb[:, 0:w]), r=[kb], w=[kb])
        yield 3 * cst_
        p.pool(lambda: nc.gpsimd.tensor_tensor(out=dst_fn(), in0=fa[:, 0:w], in1=fb[:, 0:w], op=ALU.mult),
               r=[ka, kb], w=dkeys)
        yield cst_

    def g_mixer(self, tt, l):
        nc, p, T, ws = self.nc, self.p, self.T, self.wsM
        L, S, soff = self.L, self.S, self.soff
        NKB = S // 128
        t0 = tt * TT
        PS, xn, yT = T["PS"], T["xnM"], T["yT"]
        xT = T["xT"][tt % 2]
        XP = tt % 2
        small, cst = T["small"], T["cst"]
        F, B, E = T["F"], T["B"], T["E"]
        ONESNEG = lambda: cst[:, 128:256]
        TRINEG = lambda: cst[:, 256:384]
        IDENT = lambda: cst[:, 384:512]
        NEGMASK = lambda: cst[:, 512:640]
        C0 = {"pool": 0, "q": 256, "k": 512, "v": 768, "su": 1024, "sv": 1280, "lx": 1536, "lg": 1792}
        kc_t, vc_t, qz = T["kc"], T["vc"], T["qz"]
        wins = (2, 4, 8, 16)
        OB, NB_, ACC = 3, 4, 2
        gen = self

        ws.begin_phase(("M", l))
        yield from self.rmsnorm(tt, "gm", l, xn, "xnM", NB_, F[6], ("F", 6))

        for c in range(2):
            bank = self.proj_fm(l, C0["k"] + c * 128)
            p.dve(lambda bank=bank, c=c: nc.vector.tensor_copy(out=kc_t[:, l * 2 + c, t0:t0 + TT], in_=PS[bank][:, :]),
                  r=[("ps", bank)], w=[("kc", l, c)])
            yield 2.1
            bank = self.proj_fm(l, C0["q"] + c * 128)
            for hf in range(2):
                h_ = 2 * c + hf
                sl = slice(64 * hf, 64 * hf + 64)
                p.act(lambda bank=bank, h_=h_, sl=sl: nc.scalar.activation(
                    out=qz[h_][sl, :], in_=PS[bank][sl, :], func=AF.Copy, scale=0.125),
                    r=[("ps", bank)], w=[("qz", h_)])
            yield 2.1
        waps = self.proj_tm_blocks(l, C0["v"])
        for tb in range(4):
            half = tb % 2
            self.proj_tm_tb(waps, tb, OB, half)
            kb = tt * 4 + tb
            p.dve(lambda half=half, kb=kb: nc.vector.tensor_copy(
                out=vc_t[:, l * NKB + kb, :], in_=PS[OB][:, half * 256:half * 256 + 256]),
                r=[("ps", OB)], w=[("vc", l)])
            yield 1.1

        nkb = 4 * tt + 4

        def g_attn():
            for hp in range(2):
                hs = (2 * hp, 2 * hp + 1)
                ulist = [(kb, h) for kb in range(nkb - 1, -1, -1) for h in hs]
                state = {"ei": 0}

                def stage1(u):
                    kb, h = u
                    c, pb = h // 2, 64 * (h % 2)
                    j = kb - 4 * tt
                    c0 = max(0, 128 * j)
                    zb = gen.zrr % 2
                    gen.zrr += 1
                    ei = state["ei"]
                    state["ei"] = ei + 1
                    e2 = ei % 2
                    ef, spb, ab = E[e2], B[2 + e2], B[4 + e2]
                    state[u] = (zb, ef, spb, ab, e2, c0)
                    p.pe(lambda: nc.tensor.matmul(
                        PS[zb][:, c0:TT], lhsT=kc_t[:, l * 2 + c, kb * 128:(kb + 1) * 128],
                        rhs=qz[h][:, c0:TT], start=True, stop=False, skip_group_check=True),
                        r=[("kc", l, c), ("qz", h)], w=[("ps", zb)])
                    if j >= 0:
                        p.pe(lambda: nc.tensor.matmul(
                            PS[zb][:, c0:c0 + 128], lhsT=IDENT(), rhs=NEGMASK(), start=False, stop=False,
                            skip_group_check=True),
                            r=["cst"], w=[("ps", zb)])
                    p.act(lambda: nc.scalar.activation(out=ef[:, c0:TT], in_=PS[zb][:, c0:TT], func=AF.Exp),
                          r=[("ps", zb)], w=[("E", e2)])
                    p.act(lambda: nc.scalar.activation(out=spb[:, c0:TT], in_=ef[:, c0:TT], func=AF.Ln,
                                                       bias=1.0, scale=1.0),
                          r=[("E", e2)], w=[("B", 2 + e2)])

                def stage2(u):
                    kb, h = u
                    zb, ef, spb, ab, e2, c0 = state.pop(u)
                    j = kb - 4 * tt
                    first = (kb == nkb - 1)
                    hh = h % 2
                    Sh = T["S"][hh]
                    S32 = T["S32"][hh]
                    cs = c0 + 128 if j >= 0 else 0
                    p.pe(lambda: nc.tensor.matmul(
                        PS[zb][:, c0:TT], lhsT=TRINEG(), rhs=spb[:, c0:TT], start=False, stop=first,
                        skip_group_check=True),
                        r=["cst", ("B", 2 + e2)], w=[("ps", zb)])
                    if not first:
                        p.pe(lambda: nc.tensor.matmul(
                            PS[zb][:, cs:TT], lhsT=ONESNEG(), rhs=Sh[:, cs:TT], start=False, stop=True,
                            skip_group_check=True),
                            r=["cst", ("S", hh)], w=[("ps", zb)])
                    if kb > 0:
                        if first:
                            p.pool(lambda: nc.gpsimd.tensor_copy(out=S32[:, c0:TT], in_=spb[:, c0:TT]),
                                   r=[("B", 2 + e2)], w=[("S32", hh)])
                        else:
                            p.pool(lambda: nc.gpsimd.tensor_tensor(out=S32[:, cs:TT], in0=S32[:, cs:TT],
                                                                   in1=spb[:, cs:TT], op=ALU.add),
                                   r=[("B", 2 + e2), ("S32", hh)], w=[("S32", hh)])
                            if j >= 0:
                                p.pool(lambda: nc.gpsimd.tensor_copy(out=S32[:, c0:cs], in_=spb[:, c0:cs]),
                                       r=[("B", 2 + e2)], w=[("S32", hh)])
                        p.dve(lambda: nc.vector.tensor_copy(out=Sh[:, c0:TT], in_=S32[:, c0:TT]),
                              r=[("S32", hh)], w=[("S", hh)])
                    p.act(lambda: nc.scalar.activation(out=ab[:, c0:TT], in_=PS[zb][:, c0:TT], func=AF.Exp),
                          r=[("ps", zb)], w=[("B", 4 + e2)])
                    state[("s3", u)] = (ab, e2, c0)

                def stage3(u):
                    kb, h = u
                    c, pb = h // 2, 64 * (h % 2)
                    ab, e2, c0 = state.pop(("s3", u))
                    first = (kb == nkb - 1)
                    ak = (ACC, NB_)[h % 2]
                    p.pe(lambda: nc.tensor.matmul(
                        PS[ak][:, c0:TT], lhsT=vc_t[:, l * NKB + kb, c * 128:(c + 1) * 128],
                        rhs=ab[:, c0:TT], start=first, stop=(kb == 0), skip_group_check=True),
                        r=[("vc", l), ("B", 4 + e2)], w=[("ps", ak)])
                    if kb == 0:
                        p.dve(lambda: nc.vector.tensor_copy(out=yT[pb:pb + 64, 2 + c, :],
                                                            in_=PS[ak][pb:pb + 64, :]),
                              r=[("ps", ak)], w=[("yT", 2 + c)])

                n = len(ulist)
                for i in range(n + 2):
                    if i < n:
                        stage1(ulist[i])
                    if 1 <= i <= n:
                        stage2(ulist[i - 1])
                    if i >= 2:
                        stage3(ulist[i - 2])
                    yield 0.9

        def g_pool(c):
            X, S2, S4 = F[2], F[3], F[4]
            kX, k2, k4 = ("F", 2), ("F", 3), ("F", 4)
            hidx = l * 2 + c
            W = TT + 16
            p.dve(lambda: nc.vector.tensor_copy(out=X[:, 0:16], in_=T["poolh"][:, hidx, :]),
                  r=[("poolh", hidx)], w=[kX])
            self.proj_fm(l, C0["pool"] + c * 128)
            yield 2.1
            p.dve(lambda: nc.vector.tensor_copy(out=X[:, 16:16 + TT], in_=PS[OB][:, :]),
                  r=[("ps", OB)], w=[kX])
            yield 0.5
            p.dve(lambda: nc.vector.tensor_copy(out=T["poolh"][:, hidx, :], in_=X[:, TT:TT + 16]),
                  r=[kX], w=[("poolh", hidx)])
            p.pool(lambda: nc.gpsimd.tensor_tensor(out=S2[:, 1:W], in0=X[:, 1:W], in1=X[:, 0:W - 1], op=ALU.add),
                   r=[kX], w=[k2])
            yield 0.5
            p.pool(lambda: nc.gpsimd.tensor_tensor(out=S4[:, 3:W], in0=S2[:, 3:W], in1=S2[:, 1:W - 2], op=ALU.add),
                   r=[k2], w=[k4])
            yield 0.5
            if c == 0:
                lvl, lk = (S2, S4), (k2, k4)
            else:
                S8, S16 = F[5], F[3]
                k8, k16 = ("F", 5), ("F", 3)
                p.pool(lambda: nc.gpsimd.tensor_tensor(out=S8[:, 7:W], in0=S4[:, 7:W], in1=S4[:, 3:W - 4],
                                                       op=ALU.add), r=[k4], w=[k8])
                yield 0.5
                p.pool(lambda: nc.gpsimd.tensor_tensor(out=S16[:, 15:W], in0=S8[:, 15:W], in1=S8[:, 7:W - 8],
                                                       op=ALU.add), r=[k8], w=[k16])
                yield 0.5
                lvl, lk = (S8, S16), (k8, k16)
            db = B[0]
            for hf in range(2):
                g = 2 * c + hf
                sl = slice(64 * hf, 64 * hf + 64)
                Lv, kk = lvl[hf], lk[hf]
                p.dve(lambda Lv=Lv, sl=sl, g=g: nc.vector.scalar_tensor_tensor(
                    out=db[sl, :], in0=Lv[sl, 16:16 + TT], scalar=1.0 / wins[g], op0=ALU.mult,
                    in1=X[sl, 16:16 + TT], op1=ALU.subtract),
                    r=[kk, kX], w=[("B", 0)])
                if tt == 0:
                    p.dve(lambda Lv=Lv, sl=sl: nc.vector.tensor_tensor(
                        out=Lv[sl, 16:32], in0=Lv[sl, 16:32], in1=T["icnt"][sl, c, :], op=ALU.mult),
                        r=[kk, "icnt"], w=[kk])
                    p.dve(lambda Lv=Lv, sl=sl: nc.vector.tensor_tensor(
                        out=db[sl, 0:16], in0=Lv[sl, 16:32], in1=X[sl, 16:32], op=ALU.subtract),
                        r=[kk, kX], w=[("B", 0)])
                yield 0.5
            p.pe(lambda: nc.tensor.matmul(PS[OB][:, :], lhsT=T["bd"][:, l * 6 + c, :], rhs=db[:, :],
                                          start=True, stop=True),
                 r=["bd", ("B", 0)], w=[("ps", OB)])
            po = soff[("pscale", l)]
            yield 0.3
            p.dve(lambda: nc.vector.tensor_scalar(
                out=yT[:, c, :], in0=PS[OB][:, :], scalar1=small[:, po + c:po + c + 1], scalar2=None,
                op0=ALU.mult),
                r=[("ps", OB), "small"], w=[("yT", c)])
            yield 0.5

        def g_sgu():
            ug = [F[2], F[3]]
            for c in range(2):
                self.proj_fm(l, C0["su"] + c * 128)
                yield 2.1
                yield from self.gelu(lambda c=c: ug[c][:, 0:TT], [("F", 2 + c)], lambda: PS[OB][:, :],
                                     [("ps", OB)], 4, 5, TT)
            waps = self.proj_tm_blocks(l, C0["sv"])
            stt = T["st"]
            ks = "st"
            vg = F[4]
            kv = ("F", 4)
            vn = B[1]
            for tb in range(4):
                half = tb % 2
                self.proj_tm_tb(waps, tb, OB, half)
                yield 1.1
                yield from self.gelu(lambda: vg[:, 0:256], [kv],
                                     lambda half=half: PS[OB][:, half * 256:half * 256 + 256],
                                     [("ps", OB)], 0, 1, 256)
                p.dve(lambda: nc.vector.tensor_reduce(
                    out=stt[:, 8:12], in_=vg[:, 0:256].rearrange("p (h d) -> p h d", h=4), axis=AX.X, op=ALU.add),
                    r=[kv], w=[ks])
                p.pool(lambda: nc.gpsimd.tensor_tensor(out=vg[:, 256:512], in0=vg[:, 0:256], in1=vg[:, 0:256],
                                                       op=ALU.mult), r=[kv], w=[kv])
                yield 0.5
                p.dve(lambda: nc.vector.tensor_reduce(
                    out=stt[:, 12:16], in_=vg[:, 256:512].rearrange("p (h d) -> p h d", h=4), axis=AX.X,
                    op=ALU.add), r=[kv], w=[ks])
                p.dve(lambda: nc.vector.tensor_scalar(out=stt[:, 8:12], in0=stt[:, 8:12], scalar1=1.0 / 64,
                                                      scalar2=None, op0=ALU.mult), r=[ks], w=[ks])
                p.dve(lambda: nc.vector.tensor_tensor(out=stt[:, 16:20], in0=stt[:, 8:12], in1=stt[:, 8:12],
                                                      op=ALU.mult), r=[ks], w=[ks])
                p.dve(lambda: nc.vector.scalar_tensor_tensor(out=stt[:, 12:16], in0=stt[:, 12:16], scalar=1.0 / 64,
                                                             op0=ALU.mult, in1=stt[:, 16:20], op1=ALU.subtract),
                      r=[ks], w=[ks])
                yield 0.4
                p.act(lambda: nc.scalar.activation(out=stt[:, 12:16], in_=stt[:, 12:16], func=AF.Ln,
                                                   bias=T["epsb"][:, 0:1], scale=1.0), r=[ks, "epsb"], w=[ks])
                p.act(lambda: nc.scalar.activation(out=stt[:, 12:16], in_=stt[:, 12:16], func=AF.Exp, scale=-0.5),
                      r=[ks], w=[ks])
                yield 0.3
                for h in range(4):
                    p.dve(lambda h=h: nc.vector.tensor_scalar(
                        out=vn[:, h * 64:(h + 1) * 64], in0=vg[:, h * 64:(h + 1) * 64],
                        scalar1=stt[:, 8 + h:9 + h], scalar2=stt[:, 12 + h:13 + h], op0=ALU.subtract, op1=ALU.mult),
                        r=[kv, ks], w=[("B", 1)])
                yield 0.4
                rg = (tb % 2) * 256
                for h in range(4):
                    c, pb = h // 2, 64 * (h % 2)
                    p.pe(lambda h=h, c=c, pb=pb, rg=rg: nc.tensor.matmul(
                        PS[OB][pb:pb + 64, rg + c * 128:rg + c * 128 + 128], lhsT=vn[:, h * 64:(h + 1) * 64],
                        rhs=T["wsT"][:, l * 4 + h, :], start=True, stop=True, skip_group_check=True),
                        r=[("B", 1), "wsT"], w=[("ps", OB)])
                yield 0.3
                for c in range(2):
                    sl = slice(tb * 128, (tb + 1) * 128)
                    p.dve(lambda c=c, sl=sl, rg=rg: nc.vector.tensor_tensor(
                        out=F[5][:, c * 128:c * 128 + 128], in0=PS[OB][:, rg + c * 128:rg + c * 128 + 128],
                        in1=T["bsb"][:, l * 2 + c, :], op=ALU.add),
                        r=[("ps", OB), "bsb"], w=[("F", 5)])
                    p.pool(lambda c=c, sl=sl: nc.gpsimd.tensor_tensor(
                        out=yT[:, 4 + c, sl], in0=F[5][:, c * 128:c * 128 + 128], in1=ug[c][:, sl], op=ALU.mult),
                        r=[("F", 5), ("F", 2 + c)], w=[("yT", 4 + c)])
                yield 0.4

        def g_lru(c):
            XB, XC, R, I = F[2], F[3], F[4], F[5]
            kXB, kXC, kR, kI = ("F", 2), ("F", 3), ("F", 4), ("F", 5)
            A_, TH = F[0], F[1]
            kA, kTH = ("F", 0), ("F", 1)
            hidx = l * 2 + c
            p.dve(lambda: nc.vector.tensor_copy(out=XB[:, 0:3], in_=T["lruh"][:, hidx, :]),
                  r=[("lruh", hidx)], w=[kXB])
            self.proj_fm(l, C0["lx"] + c * 128)
            yield 2.1
            p.dve(lambda: nc.vector.tensor_copy(out=XB[:, 3:3 + TT], in_=PS[OB][:, :]),
                  r=[("ps", OB)], w=[kXB])
            yield 0.5
            p.dve(lambda: nc.vector.tensor_copy(out=T["lruh"][:, hidx, :], in_=XB[:, TT:TT + 3]),
                  r=[kXB], w=[("lruh", hidx)])
            cw = soff[("convw", l)] + c * 4
            cb = soff[("convb", l)] + c
            p.dve(lambda: nc.vector.tensor_scalar(out=XC[:, 0:TT], in0=XB[:, 3:3 + TT],
                                                  scalar1=small[:, cw + 3:cw + 4], scalar2=small[:, cb:cb + 1],
                                                  op0=ALU.mult, op1=ALU.add),
                  r=[kXB, "small"], w=[kXC])
            yield 0.5
            for j in range(3):
                p.dve(lambda j=j: nc.vector.scalar_tensor_tensor(
                    out=XC[:, 0:TT], in0=XB[:, j:j + TT], scalar=small[:, cw + j:cw + j + 1], op0=ALU.mult,
                    in1=XC[:, 0:TT], op1=ALU.add),
                    r=[kXB, kXC, "small"], w=[kXC])
                yield 0.5
            xcb = B[0]
            p.pool(lambda: nc.gpsimd.tensor_copy(out=xcb[:, :], in_=XC[:, 0:TT]), r=[kXC], w=[("B", 0)])
            yield 0.5
            nbo = l * 4 + c
            for (dst, kd, wi_, bo) in ((R, kR, 2, nbo), (I, kI, 4, nbo + 2)):
                p.pe(lambda wi_=wi_: nc.tensor.matmul(
                    PS[OB][:, :], lhsT=T["bd"][:, l * 6 + wi_ + c, :], rhs=xcb[:, :], start=True, stop=True),
                    r=["bd", ("B", 0)], w=[("ps", OB)])
                yield 0.3
                p.act(lambda dst=dst, bo=bo: nc.scalar.activation(
                    out=dst[:, 0:TT], in_=PS[OB][:, :], func=AF.Exp, bias=T["nb"][:, bo:bo + 1], scale=-1.0),
                    r=[("ps", OB), "nb"], w=[kd])
                yield 0.5
                p.pool(lambda dst=dst: nc.gpsimd.tensor_scalar(out=dst[:, 0:TT], in0=dst[:, 0:TT], scalar1=1.0,
                                                               scalar2=1.0, op0=ALU.mult, op1=ALU.add),
                       r=[kd], w=[kd])
                yield 0.5
                p.dve(lambda dst=dst: nc.vector.reciprocal(out=dst[:, 0:TT], in_=dst[:, 0:TT]), r=[kd], w=[kd])
                yield 1.5
            p.act(lambda: nc.scalar.activation(out=A_[:, 0:TT], in_=R[:, 0:TT], func=AF.Exp,
                                               scale=T["cneg"][:, l * 2 + c:l * 2 + c + 1]),
                  r=[kR, "cneg"], w=[kA])
            p.act(lambda: nc.scalar.activation(out=TH[:, 0:TT], in_=R[:, 0:TT], func=AF.Tanh,
                                               scale=T["cneg"][:, 2 * L + l * 2 + c:2 * L + l * 2 + c + 1]),
                  r=[kR, "cneg"], w=[kTH])
            yield 1.0
            p.pool(lambda: nc.gpsimd.tensor_tensor(out=I[:, 0:TT], in0=I[:, 0:TT], in1=XC[:, 0:TT], op=ALU.mult),
                   r=[kI, kXC], w=[kI])
            yield 0.5
            p.pool(lambda: nc.gpsimd.tensor_tensor(out=R[:, 0:TT], in0=A_[:, 0:TT], in1=A_[:, 0:TT], op=ALU.mult),
                   r=[kA, kTH], w=[kR])
            yield 0.5
            p.dve(lambda: nc.vector.scalar_tensor_tensor(out=R[:, 0:TT], in0=R[:, 0:TT], scalar=1.0, op0=ALU.add,
                                                         in1=TH[:, 0:TT], op1=ALU.mult),
                  r=[kR, kTH], w=[kR])
            p.dve(lambda: nc.vector.tensor_scalar(out=R[:, 0:TT], in0=R[:, 0:TT], scalar1=1e-30, scalar2=None,
                                                  op0=ALU.max), r=[kR], w=[kR])
            yield 1.0
            p.act(lambda: nc.scalar.activation(out=R[:, 0:TT], in_=R[:, 0:TT], func=AF.Ln), r=[kR], w=[kR])
            p.act(lambda: nc.scalar.activation(out=R[:, 0:TT], in_=R[:, 0:TT], func=AF.Exp, scale=0.5),
                  r=[kR], w=[kR])
            yield 1.0
            p.dve(lambda: nc.vector.tensor_tensor(out=I[:, 0:TT], in0=I[:, 0:TT], in1=R[:, 0:TT], op=ALU.mult),
                  r=[kI, kR], w=[kI])
            yield 0.5
            p.dve(lambda: nc.vector.tensor_tensor_scan(
                out=XC[:, 0:TT], data0=A_[:, 0:TT], data1=I[:, 0:TT], initial=T["hst"][:, hidx:hidx + 1],
                op0=ALU.mult, op1=ALU.add),
                r=[kA, kI, ("hst", hidx)], w=[kXC])
            p.dve(lambda: nc.vector.tensor_copy(out=T["hst"][:, hidx:hidx + 1], in_=XC[:, TT - 1:TT]),
                  r=[kXC], w=[("hst", hidx)])
            yield 1.0
            self.proj_fm(l, C0["lg"] + c * 128)
            yield 2.1
            yield from self.gelu(lambda: F[4][:, 0:TT], [("F", 4)], lambda: PS[OB][:, :], [("ps", OB)], 0, 1, TT)
            p.dve(lambda: nc.vector.tensor_tensor(out=yT[:, 6 + c, :], in0=XC[:, 0:TT], in1=F[4][:, 0:TT],
                                                  op=ALU.mult),
                  r=[kXC, ("F", 4)], w=[("yT", 6 + c)])
            yield 0.5

        def g_acd():
            yield from g_pool(0)
            yield from g_pool(1)
            yield from g_sgu()
            yield from g_lru(0)
            yield from g_lru(1)

        ga, gb = g_attn(), g_acd()
        va = vb = 0.0
        alive_a = alive_b = True
        while alive_a or alive_b:
            if alive_a and (va <= vb or not alive_b):
                try:
                    c_ = next(ga)
                    va += c_
                    if va > vb or not alive_b:
                        yield c_
                except StopIteration:
                    alive_a = False
            else:
                try:
                    c_ = next(gb)
                    vb += c_
                    if vb > va or not alive_a:
                        yield c_
                except StopIteration:
                    alive_b = False

        for dc in range(8):
            for kc in range(8):
                wap, wk = ws.get("mix_w_out", l, kc * 128, dc * 128)
                p.pe(lambda wap=wap, kc=kc: nc.tensor.matmul(
                    PS[OB][:, :], lhsT=wap, rhs=yT[:, kc, :], start=(kc == 0), stop=(kc == 7)),
                    r=[wk, ("yT", kc)], w=[("ps", OB)])
            p.dve(lambda dc=dc: nc.vector.tensor_tensor(
                out=xT[:, dc, :], in0=PS[OB][:, :], in1=xT[:, dc, :], op=ALU.add),
                r=[("ps", OB), ("xT", XP, dc)], w=[("xT", XP, dc)])
            yield 2.1
        ws.end_phase()


_CACHE = {}


def _get_prog(S, L, final):
    key = (S, L, final)
    if key not in _CACHE:
        _CACHE[key] = build(S, L, final)
    return _CACHE[key]


def _pack_weights(ws, W, lsel):
    CH = ws.CH
    wst = np.zeros((ws.total_chunks, 128, CH), np.float32)
    for key, ph in ws.phases.items():
        base = ph["base"]
        for (name, l, r0, c0, ncols, pos) in ph["blocks"]:
            ch, off = base + pos // CH, pos % CH
            wst[ch, :, off:off + ncols] = W[name][lsel[l], r0:r0 + 128, c0:c0 + ncols]
    return wst


def _run(xT_list, W, lsel, final):
    S = xT_list[0].shape[1]
    L = len(lsel)
    nc, packs, NS, soff = _get_prog(S, L, final)
    wstF = _pack_weights(packs["F"], W, lsel)
    wstM = _pack_weights(packs["M"], W, lsel)
    small = np.zeros((128, NS), np.float32)
    for li, l in enumerate(lsel):
        small[:, soff[("g1", li)]:soff[("g1", li)] + 8] = _vec8(W["ffn1_norm"][l])
        small[:, soff[("gm", li)]:soff[("gm", li)] + 8] = _vec8(W["mix_norm"][l])
        small[:, soff[("g2", li)]:soff[("g2", li)] + 8] = _vec8(W["ffn2_norm"][l])
        small[:, soff[("pscale", li)]:soff[("pscale", li)] + 2] = _vec2(W["pool_scale"][l])
        cw = W["conv_w"][l]
        for c in range(2):
            for j in range(4):
                small[:, soff[("convw", li)] + c * 4 + j] = cw[j, c * 128:(c + 1) * 128]
        small[:, soff[("convb", li)]:soff[("convb", li)] + 2] = _vec2(W["conv_b"][l])
        small[:, soff[("ba", li)]:soff[("ba", li)] + 2] = _vec2(W["lru_ba"][l])
        small[:, soff[("bx", li)]:soff[("bx", li)] + 2] = _vec2(W["lru_bx"][l])
        small[:, soff[("lam", li)]:soff[("lam", li)] + 2] = _vec2(W["lru_lambda"][l])
    if final:
        small[:, soff[("gf", 0)]:soff[("gf", 0)] + 8] = _vec8(W["final_norm"])
    bd = np.zeros((L, 6, 128, 128), np.float32)
    wsT = np.zeros((L, 4, 128, 128), np.float32)
    bsb = np.zeros((L, 2, 128, 128), np.float32)
    for li, l in enumerate(lsel):
        for c in range(2):
            bd[li, c] = _blockdiag(W["pool_w"][l, 2 * c:2 * c + 2])
            bd[li, 2 + c] = _blockdiag(W["lru_wa"][l, 2 * c:2 * c + 2])
            bd[li, 4 + c] = _blockdiag(W["lru_wx"][l, 2 * c:2 * c + 2])
            for hf in range(2):
                bsb[li, c, 64 * hf:64 * hf + 64, :] = W["sgu_b"][l, 2 * c + hf][None, :]
        for h in range(4):
            wsT[li, h] = W["sgu_w"][l, h].T
    cst = _consts()
    shared = {"wstF": wstF, "wstM": wstM, "small": small, "cst": cst, "bd": bd, "wsT": wsT, "bsb": bsb}
    in_maps = [dict(shared, xT=np.ascontiguousarray(x)) for x in xT_list]
    n = len(xT_list)
    res = run_bass_kernel_spmd(nc, in_maps, core_ids=list(range(n)))
    return [r["yT"] for r in res.results]


def kernel(**inputs):
    W = {k: np.asarray(v, dtype=np.float32) for k, v in inputs.items()}
    x = W.pop("x")
    Bn = x.shape[0]
    xT = [np.ascontiguousarray(x[b].T) for b in range(Bn)]
    depth = W["ffn1_norm"].shape[0]
    outs = _run(xT, W, list(range(depth)), True)
    return np.stack([o.T for o in outs], axis=0).astype(np.float32)
```
